# Optimizing a Trainium2 kernel written in Bass

```python
import math
import jax, jax.numpy as jnp
from jax import lax
import numpy as np

D_MODEL = 1024
BATCH = 8
SEQ = 8192
DEPTH = 4
DEC_BATCH = 4
DEC_SEQ = 4096
PAST_LEN = 128

N_EVEN = (DEPTH + 1) // 2
N_ODD = DEPTH // 2
D_A = D_MODEL // 2
D_B = D_MODEL // 2
A_HEADS = 4
A_DV = D_A // A_HEADS
A_DK = A_DV // 2
B_ORDER = 2
D_IN_AB = 3 * D_A + (B_ORDER + 1) * D_B
Q_BLOCK = 128
REL_BUCKETS = 32
REL_MAX_DIST = 128
FILTER_EMB = 33
FILTER_BANDS = (FILTER_EMB - 1) // 2
FILTER_WIDTH = 64
FILTER_CH = 2 * B_ORDER * D_B
HY_TARGET = 1e-2
HY_FAST = 0.3
HY_SLOW = 1.5
C_GROUPS = 4
C_GROUP_W = D_MODEL // C_GROUPS
N_EXPERTS = 16
EC_FACTOR = 2
D_FF_E = 2816
LN_EPS = 1e-5
DN_ALPHA = (2 * DEPTH) ** 0.25
DN_BETA = (8 * DEPTH) ** -0.25

kernel_name = "hybrid_diffattn_hyena_fnet_ec_encoder"


def layer_norm(x, g, b):
    xf = x.astype(jnp.float32)
    mu = jnp.mean(xf, -1, keepdims=True)
    var = jnp.mean(jnp.square(xf - mu), -1, keepdims=True)
    return ((xf - mu) * lax.rsqrt(var + LN_EPS) * g.astype(jnp.float32) + b.astype(jnp.float32)).astype(x.dtype)


def rel_bucket(rel):
    nb = REL_BUCKETS // 2
    max_exact = nb // 2
    n = jnp.abs(rel)
    large = max_exact + (jnp.log(jnp.maximum(n, 1).astype(jnp.float32) / max_exact)
                         / math.log(REL_MAX_DIST / max_exact) * (nb - max_exact)).astype(jnp.int32)
    large = jnp.minimum(large, nb - 1)
    return jnp.where(rel > 0, nb, 0) + jnp.where(n < max_exact, n, large)


def diff_attention(q, k, v, lam, sub_g, rel_table, lambda_init):
    bsz, L = q.shape[0], q.shape[1]
    n_blk = L // Q_BLOCK
    lf = lam.astype(jnp.float32)
    lam_full = jnp.exp(jnp.sum(lf[0] * lf[1])) - jnp.exp(jnp.sum(lf[2] * lf[3])) + lambda_init
    scale = A_DK ** -0.5
    kpos = jnp.arange(L)
    qb = q.reshape(bsz, n_blk, Q_BLOCK, A_HEADS, 2, A_DK).transpose(1, 0, 2, 3, 4, 5)
    tab = rel_table.astype(jnp.float32)

    def block(args):
        i, qi = args
        s = jnp.einsum('bqhmd,bkhmd->bhmqk', qi, k, preferred_element_type=jnp.float32) * scale
        qpos = i * Q_BLOCK + jnp.arange(Q_BLOCK)
        bias = tab[rel_bucket(kpos[None, :] - qpos[:, None])]
        s = s + bias.transpose(2, 0, 1)[None, :, None]
        p = jax.nn.softmax(s, axis=-1)
        w = p[:, :, 0] - lam_full * p[:, :, 1]
        return jnp.einsum('bhqk,bkhd->bqhd', w.astype(v.dtype), v)

    o = lax.map(block, (jnp.arange(n_blk), qb))
    o = o.transpose(1, 0, 2, 3, 4).reshape(bsz, L, A_HEADS, A_DV).astype(jnp.float32)
    o = o * lax.rsqrt(jnp.mean(o * o, -1, keepdims=True) + LN_EPS) * sub_g.astype(jnp.float32) * (1.0 - lambda_init)
    return o.reshape(bsz, L, D_A).astype(q.dtype)


def short_conv(u, w, b):
    L = u.shape[1]
    up = jnp.pad(u, ((0, 0), (1, 1), (0, 0)))
    return up[:, :L] * w[0] + up[:, 1:L + 1] * w[1] + up[:, 2:] * w[2] + b


def hyena_filters(L, w1, b1, w2, b2, w3, freq, decay):
    f32 = jnp.float32
    t = jnp.linspace(0.0, 1.0, L, dtype=f32)[:, None]
    wpos = 2.0 * math.pi * jnp.arange(L, dtype=f32)[:, None] / L
    fr = jnp.linspace(1e-4, FILTER_BANDS - 1, FILTER_BANDS, dtype=f32)[None, :]
    z = jnp.concatenate([t, jnp.cos(fr * wpos), -jnp.sin(fr * wpos)], -1)
    fq = freq.astype(f32)
    h = jnp.sin(fq[0] * (z @ w1.astype(f32) + b1.astype(f32)))
    h = jnp.sin(fq[1] * (h @ w2.astype(f32) + b2.astype(f32)))
    h = (h @ w3.astype(f32)) * jnp.exp(-t * jnp.abs(decay.astype(f32)))
    h = h.reshape(L, 2, B_ORDER, D_B)
    return h / jnp.sum(jnp.abs(h), axis=(0, 1), keepdims=True)


def hyena_mixer(u, conv_w, conv_b, w1, b1, w2, b2, w3, freq, decay, skip):
    L = u.shape[1]
    uc = short_conv(u, conv_w, conv_b).astype(jnp.float32)
    v, x1, x2 = jnp.split(uc, 3, axis=-1)
    h = hyena_filters(L, w1, b1, w2, b2, w3, freq, decay)
    k2 = jnp.concatenate([h[:, 0], jnp.zeros((1, B_ORDER, D_B), jnp.float32), h[:0:-1, 1]], axis=0)
    kf = jnp.fft.rfft(k2, axis=0)
    sk = skip.astype(jnp.float32)
    z = v
    for n, gate in enumerate((x1, x2)):
        zf = jnp.fft.rfft(z, n=2 * L, axis=1)
        z = gate * (jnp.fft.irfft(zf * kf[None, :, n], n=2 * L, axis=1)[:, :L] + z * sk[n])
    return z.astype(u.dtype)


def fourier_mix(x, w_out):
    bsz, L, d = x.shape
    xg = x.astype(jnp.float32).reshape(bsz, L, C_GROUPS, C_GROUP_W)
    f = jnp.fft.fftn(xg, axes=(1, 3), norm='ortho').real
    return f.reshape(bsz, L, d).astype(x.dtype) @ w_out


def expert_choice(x, w_router, w_gate, w_up, w_down):
    bsz, L, d = x.shape
    T = bsz * L
    cap = max(1, EC_FACTOR * T // N_EXPERTS)
    xt = x.reshape(T, d)
    aff = jax.nn.softmax((xt @ w_router).astype(jnp.float32), axis=-1)
    gate, idx = lax.top_k(aff.T, cap)

    def expert(args):
        ids, g, wg, wu, wd = args
        xe = xt[ids]
        he = jax.nn.silu(xe @ wg) * (xe @ wu)
        return (he @ wd) * g[:, None].astype(x.dtype)

    out = lax.map(expert, (idx, gate, w_gate, w_up, w_down))
    y = jnp.zeros_like(xt).at[idx.reshape(-1)].add(out.reshape(-1, d))
    return y.reshape(bsz, L, d)


def trunk(x, rel_bias, ab_w_in, ab_w_out, diff_lambda, diff_subln_g, hy_conv_w, hy_conv_b,
          hy_f_w1, hy_f_b1, hy_f_w2, hy_f_b2, hy_f_w3, hy_f_freq, hy_decay, hy_skip,
          c_w_out, ec_router, ec_w_gate, ec_w_up, ec_w_down, ln_g, ln_b):
    bsz, L, _ = x.shape
    for layer in range(DEPTH):
        j = layer // 2
        if layer % 2 == 0:
            hproj = x @ ab_w_in[j]
            q = hproj[..., :D_A].reshape(bsz, L, A_HEADS, 2, A_DK)
            k = hproj[..., D_A:2 * D_A].reshape(bsz, L, A_HEADS, 2, A_DK)
            v = hproj[..., 2 * D_A:3 * D_A].reshape(bsz, L, A_HEADS, A_DV)
            lambda_init = 0.8 - 0.6 * math.exp(-0.3 * layer)
            a_out = diff_attention(q, k, v, diff_lambda[j], diff_subln_g[j], rel_bias, lambda_init)
            b_out = hyena_mixer(hproj[..., 3 * D_A:], hy_conv_w[j], hy_conv_b[j], hy_f_w1[j], hy_f_b1[j],
                                hy_f_w2[j], hy_f_b2[j], hy_f_w3[j], hy_f_freq[j], hy_decay[j], hy_skip[j])
            mix = jnp.concatenate([a_out, b_out], axis=-1) @ ab_w_out[j]
        else:
            mix = fourier_mix(x, c_w_out[j])
        x = layer_norm(DN_ALPHA * x + mix, ln_g[layer, 0], ln_b[layer, 0])
        ffn = expert_choice(x, ec_router[layer], ec_w_gate[layer], ec_w_up[layer], ec_w_down[layer])
        x = layer_norm(DN_ALPHA * x + ffn, ln_g[layer, 1], ln_b[layer, 1])
    return x


def setup_inputs(seed: int = 0) -> dict:
    key = jax.random.key(seed)
    ks = jax.random.split(key, 26)
    nrm = lambda k, s, sc: jax.random.normal(k, s, jnp.float32) * sc
    min_decay = math.log(HY_TARGET) / HY_SLOW
    max_decay = math.log(HY_TARGET) / HY_FAST
    decay_base = jnp.linspace(min_decay, max_decay, FILTER_CH, dtype=jnp.float32)[None, :]
    return {
        "x_prompt": nrm(ks[0], (BATCH, SEQ, D_MODEL), 1.0),
        "x_sample": nrm(ks[1], (DEC_BATCH, DEC_SEQ, D_MODEL), 1.0),
        "rel_bias": nrm(ks[2], (REL_BUCKETS, A_HEADS), 0.2),
        "ab_w_in": nrm(ks[3], (N_EVEN, D_MODEL, D_IN_AB), D_MODEL ** -0.5),
        "ab_w_out": nrm(ks[4], (N_EVEN, D_A + D_B, D_MODEL), (D_A + D_B) ** -0.5 * DN_BETA),
        "diff_lambda": nrm(ks[5], (N_EVEN, 4, A_DK), 0.1),
        "diff_subln_g": 1.0 + nrm(ks[6], (N_EVEN, A_DV), 0.01),
        "hy_conv_w": nrm(ks[7], (N_EVEN, 3, (B_ORDER + 1) * D_B), 0.5),
        "hy_conv_b": nrm(ks[8], (N_EVEN, (B_ORDER + 1) * D_B), 0.01),
        "hy_f_w1": nrm(ks[9], (N_EVEN, FILTER_EMB, FILTER_WIDTH), FILTER_EMB ** -0.5),
        "hy_f_b1": nrm(ks[10], (N_EVEN, FILTER_WIDTH), 0.1),
        "hy_f_w2": nrm(ks[11], (N_EVEN, FILTER_WIDTH, FILTER_WIDTH), FILTER_WIDTH ** -0.5),
        "hy_f_b2": nrm(ks[12], (N_EVEN, FILTER_WIDTH), 0.1),
        "hy_f_w3": nrm(ks[13], (N_EVEN, FILTER_WIDTH, FILTER_CH), FILTER_WIDTH ** -0.5),
        "hy_f_freq": 1.0 + nrm(ks[14], (N_EVEN, 2, FILTER_WIDTH), 0.01),
        "hy_decay": decay_base + nrm(ks[15], (N_EVEN, FILTER_CH), 0.01),
        "hy_skip": nrm(ks[16], (N_EVEN, B_ORDER, D_B), 0.1),
        "c_w_out": nrm(ks[17], (N_ODD, D_MODEL, D_MODEL), D_MODEL ** -0.5 * DN_BETA),
        "ec_router": nrm(ks[18], (DEPTH, D_MODEL, N_EXPERTS), D_MODEL ** -0.5),
        "ec_w_gate": nrm(ks[19], (DEPTH, N_EXPERTS, D_MODEL, D_FF_E), D_MODEL ** -0.5),
        "ec_w_up": nrm(ks[20], (DEPTH, N_EXPERTS, D_MODEL, D_FF_E), D_MODEL ** -0.5),
        "ec_w_down": nrm(ks[21], (DEPTH, N_EXPERTS, D_FF_E, D_MODEL), D_FF_E ** -0.5 * DN_BETA),
        "ln_g": 1.0 + nrm(ks[22], (DEPTH, 2, D_MODEL), 0.01),
        "ln_b": nrm(ks[23], (DEPTH, 2, D_MODEL), 0.01),
    }


def reference(x_prompt, x_sample, rel_bias, ab_w_in, ab_w_out, diff_lambda, diff_subln_g, hy_conv_w, hy_conv_b,
              hy_f_w1, hy_f_b1, hy_f_w2, hy_f_b2, hy_f_w3, hy_f_freq, hy_decay, hy_skip,
              c_w_out, ec_router, ec_w_gate, ec_w_up, ec_w_down, ln_g, ln_b):
    y_prompt = trunk(x_prompt, rel_bias, ab_w_in, ab_w_out, diff_lambda, diff_subln_g, hy_conv_w, hy_conv_b,
                     hy_f_w1, hy_f_b1, hy_f_w2, hy_f_b2, hy_f_w3, hy_f_freq, hy_decay, hy_skip,
                     c_w_out, ec_router, ec_w_gate, ec_w_up, ec_w_down, ln_g, ln_b)
    y_sample = trunk(x_sample, rel_bias, ab_w_in, ab_w_out, diff_lambda, diff_subln_g, hy_conv_w, hy_conv_b,
                     hy_f_w1, hy_f_b1, hy_f_w2, hy_f_b2, hy_f_w3, hy_f_freq, hy_decay, hy_skip,
                     c_w_out, ec_router, ec_w_gate, ec_w_up, ec_w_down, ln_g, ln_b)
    return (y_prompt, y_sample)
```

```python
import math
from contextlib import ExitStack
import numpy as np
import ml_dtypes
import concourse.bass as bass
import concourse.mybir as mybir
from concourse.bass_utils import run_bass_kernel_spmd

F32 = mybir.dt.float32
BF16 = mybir.dt.bfloat16
I32 = mybir.dt.int32
ALU = mybir.AluOpType
AF = mybir.ActivationFunctionType
AX = mybir.AxisListType
NPBF = ml_dtypes.bfloat16


class Cfg:
    def __init__(self, LP=8192, LS=4096, DEPTH=4, NB_P=8, NB_S=4, CMAX_P=None, CMAX_S=None, DFF=2816):
        self.D = 1024
        self.LP, self.LS, self.DEPTH = LP, LS, DEPTH
        self.NB_P, self.NB_S = NB_P, NB_S
        self.NE = 16
        self.DFF = DFF
        self.T = LP + LS
        self.CAP_P = max(1, 2 * NB_P * LP // 16)
        self.CAP_S = max(1, 2 * NB_S * LS // 16)
        r128 = lambda v: ((int(v) + 127) // 128) * 128
        self.CMAX_P = CMAX_P or r128(LP / 8 * 1.125 + 64)
        self.CMAX_S = CMAX_S or r128(LS / 8 * 1.125 + 64)
        self.CTOT = self.CMAX_P + self.CMAX_S
        self.LN_EPS = 1e-5
        self.DN_ALPHA = (2 * 4) ** 0.25
        self.groups = [(0, LP), (LP, LS)]


class Res:
    __slots__ = ("w", "r", "dram")

    def __init__(self, dram=False):
        self.w = {}
        self.r = {}
        self.dram = dram


KD = 6


class Sched:
    def __init__(self, nc, stack):
        self.nc = nc
        self.eng = dict(pe=nc.tensor, act=nc.scalar, dve=nc.vector, pool=nc.gpsimd, sp=nc.sync)
        self.cnt = {k: 0 for k in self.eng}
        self.csem = {k: stack.enter_context(nc.semaphore(f"c_{k}")) for k in self.eng}
        self.dsem = {k: [stack.enter_context(nc.semaphore(f"d_{k}{i}")) for i in range(KD)]
                     for k in ("sp", "act", "pool")}
        self.dhist = {k: [] for k in self.dsem}
        self.ccsem = stack.enter_context(nc.semaphore("ccsem"))
        self.ncc = 0
        self.seen = {k: {} for k in self.eng}
        self.ninstr = 0
        self.skip = False
        self.skipset = set()

    def _wait(self, e, ev):
        sem, val, key, idx = ev
        if key == e and e == "pe":
            return
        sd = self.seen[e]
        if sd.get(sem.num, 0) >= val:
            return
        sd[sem.num] = val
        self.eng[e].wait_ge(sem, val)
        self.ninstr += 1

    @staticmethod
    def _add(d, ev):
        key = ev[2]
        if isinstance(key, tuple):
            lst = d.setdefault(key, [])
            lst.append(ev)
            if len(lst) > KD:
                del lst[0]
        else:
            d[key] = [ev]

    def _deps(self, e, reads, writes):
        for r in reads:
            for lst in r.w.values():
                for ev in lst:
                    self._wait(e, ev)
        for w in writes:
            if not w.dram:
                for lst in w.w.values():
                    for ev in lst:
                        self._wait(e, ev)
            for lst in w.r.values():
                for ev in lst:
                    self._wait(e, ev)

    def _record(self, ev, reads, writes):
        for r in reads:
            self._add(r.r, ev)
        for w in writes:
            self._add(w.w, ev)

    def op(self, e, fn, reads=(), writes=()):
        if self.skip:
            return None
        self._deps(e, reads, writes)
        self.cnt[e] += 1
        ev = (self.csem[e], self.cnt[e], e, self.cnt[e])
        fn(self.eng[e]).then_inc(self.csem[e], 1)
        self.ninstr += 1
        self._record(ev, reads, writes)
        return ev

    def dma(self, q, fn, reads=(), writes=()):
        if self.skip:
            return None
        h = self.dhist[q]
        n = len(h)
        if n >= KD:
            self._wait(q, h[n - KD])
        self._deps(q, reads, writes)
        sem = self.dsem[q][n % KD]
        ev = (sem, 16 * (n // KD + 1), ("d", q), n)
        fn(self.eng[q]).then_inc(sem, 16)
        self.ninstr += 1
        h.append(ev)
        self._record(ev, reads, writes)
        return ev

    def cc(self, fn, reads=(), writes=()):
        if self.skip:
            return None
        q = "pool"
        self._deps(q, reads, writes)
        self.ncc += 1
        ev = (self.ccsem, self.ncc, ("d", "cc"), self.ncc)
        fn(self.eng[q]).then_inc(self.ccsem)
        self.ninstr += 1
        self._record(ev, reads, writes)
        self._wait(q, ev)
        return ev

    def barrier(self):
        evs = []
        for k in self.eng:
            if self.cnt[k] > 0:
                evs.append((self.csem[k], self.cnt[k], k, self.cnt[k]))
        for q, h in self.dhist.items():
            evs.extend(h[-KD:])
        if self.ncc:
            evs.append((self.ccsem, self.ncc, ("d", "cc"), self.ncc))
        for k in self.eng:
            for ev in evs:
                self._wait(k, ev)

    def finish(self):
        self.barrier()


class Tile:
    def __init__(self, t):
        self.t = t
        self.res = Res()

    def __getitem__(self, k):
        return self.t[k]


class Ring:
    def __init__(self, tiles):
        self.tiles = tiles
        self.i = 0

    def next(self):
        t = self.tiles[self.i % len(self.tiles)]
        self.i += 1
        return t


class K:
    def __init__(self, nc, cfg, stack):
        self.nc, self.cfg = nc, cfg
        self.S = Sched(nc, stack)
        self.stack = stack
        self.dr = {}
        self.dres = {}
        self._n = 0
        self._bc = {}

    def dram(self, name, shape, dtype, kind="Internal"):
        t = self.nc.dram_tensor(name, list(shape), dtype, kind=kind)
        self.dr[name] = t
        self.dres[name] = Res(dram=True)
        return t

    def bcreg(self, value):
        if value not in self._bc:
            reg = self.nc.gpsimd.alloc_register(f"bc{value}")
            self.nc.gpsimd.reg_mov(reg, int(value))
            self._bc[value] = reg
        return self._bc[value]

    def sb(self, st, shape, dtype, name=None):
        self._n += 1
        t = st.enter_context(self.nc.sbuf_tensor(f"{name or 'sb'}_{self._n}", list(shape), dtype))
        return Tile(t)

    def sbring(self, st, n, shape, dtype, name=None):
        return Ring([self.sb(st, shape, dtype, name) for _ in range(n)])

    def ps(self, st, shape, dtype, name=None):
        self._n += 1
        t = st.enter_context(self.nc.psum_tensor(f"{name or 'ps'}_{self._n}", list(shape), dtype))
        return Tile(t)


def rel_bucket_np(rel):
    nb = 16
    max_exact = 8
    n = np.abs(rel)
    large = max_exact + (np.log(np.maximum(n, 1).astype(np.float32) / max_exact)
                         / math.log(128 / max_exact) * (nb - max_exact)).astype(np.int32)
    large = np.minimum(large, nb - 1)
    return np.where(rel > 0, nb, 0) + np.where(n < max_exact, n, large)


def fft_consts(N1, nk_in, inverse, scale=1.0):
    N = N1 * 128
    sgn = 1.0 if inverse else -1.0
    if not inverse:
        n1 = np.arange(nk_in)[:, None]
        k1 = np.arange(N1)[None, :]
        ang = sgn * 2 * np.pi * (n1 * k1 % N1) / N1
        F1re, F1im = np.cos(ang) * scale, np.sin(ang) * scale
        n2 = np.arange(128)[:, None, None]
        k1 = np.arange(N1)[None, :, None]
        k2 = np.arange(128)[None, None, :]
        ang = sgn * 2 * np.pi * ((n2 * (k1 + N1 * k2)) % N) / N
        Hre, Him = np.cos(ang), np.sin(ang)
        return dict(A=np.concatenate([F1re, F1im], 1), B=np.concatenate([-F1im, F1re], 1),
                    Hre=Hre, Him=Him, nHim=-Him)
    else:
        k2 = np.arange(128)[:, None]
        n2 = np.arange(128)[None, :]
        ang = sgn * 2 * np.pi * (k2 * n2 % 128) / 128
        Gre, Gim = np.cos(ang), np.sin(ang)
        k1 = np.arange(N1)[:, None, None]
        n2 = np.arange(128)[None, :, None]
        n1 = np.arange(nk_in)[None, None, :]
        ang = sgn * 2 * np.pi * (((n1 * k1 % N1) * 128 + n2 * k1) % N) / N
        Hre, Him = np.cos(ang) * scale, np.sin(ang) * scale
        return dict(A=np.concatenate([Gre, Gim], 1), B=np.concatenate([-Gim, Gre], 1),
                    Hre=Hre, Him=Him, nHim=-Him)


def xt_col(cfg, g):
    o, L = cfg.groups[g]
    return o + 2 * g + 1


def load_consts(k, st):
    c = {}
    S = k.S
    for name, shape, dt in (("ident_bf", [128, 128], BF16), ("ones_bf", [128, 128], BF16),
                            ("ones_f", [128, 128], F32), ("tri_bf", [128, 128], BF16),
                            ("svalid", [128, 1], F32)):
        t = k.sb(st, shape, dt, name)
        S.dma("sp", lambda e, t=t, name=name: e.dma_start(out=t[:, :], in_=k.dr[name][:, :]),
              reads=[k.dres[name]], writes=[t.res])
        c[name] = t
    z = k.sb(st, [128, 8], BF16, "zero")
    S.op("dve", lambda e: e.memset(z[:, :], 0.0), writes=[z.res])
    c["zero"] = z
    cfg = k.cfg
    XT = k.dr["XT"]
    for g in range(2):
        o, L = cfg.groups[g]
        for col in (xt_col(cfg, g) - 1, xt_col(cfg, g) + L):
            for cc in range(8):
                S.dma("sp", lambda e, col=col, cc=cc: e.dma_start(
                    out=XT[cc * 128:(cc + 1) * 128, col:col + 1], in_=z[:, 0:1], allow_slow_non_contiguous=True),
                    reads=[z.res], writes=[k.dres["XT"]])
    return c


def ln_store(k, C, bufs, pre, tok0, g, lt, gt, bt):
    S, cfg = k.S, k.cfg
    st1 = bufs["stat"].next()
    S.op("dve", lambda e: e.reduce_sum(out=st1[:, 0:1], in_=pre[:, :], axis=AX.X),
         reads=[pre.res], writes=[st1.res])
    S.op("dve", lambda e: e.tensor_scalar_mul(out=st1[:, 1:2], in0=st1[:, 0:1], scalar1=-1.0 / 1024),
         reads=[st1.res], writes=[st1.res])
    junk = bufs["junk"].next()
    S.op("act", lambda e: e.activation(out=junk[:, :], in_=pre[:, :], func=AF.Square,
                                       bias=st1[:, 1:2], scale=1.0, accum_out=st1[:, 2:3]),
         reads=[pre.res, st1.res], writes=[junk.res, st1.res])
    S.op("dve", lambda e: e.tensor_scalar(out=st1[:, 3:4], in0=st1[:, 2:3], scalar1=1.0 / 1024,
                                          scalar2=cfg.LN_EPS, op0=ALU.mult, op1=ALU.add),
         reads=[st1.res], writes=[st1.res])
    S.op("act", lambda e: e.activation(out=st1[:, 5:6], in_=st1[:, 3:4], func=AF.Sqrt),
         reads=[st1.res], writes=[st1.res])
    S.op("dve", lambda e: e.reciprocal(out=st1[:, 4:5], in_=st1[:, 5:6]),
         reads=[st1.res], writes=[st1.res])
    xo = bufs["xo"].next()
    S.op("dve", lambda e: e.tensor_scalar(out=xo[:, :], in0=pre[:, :], scalar1=st1[:, 1:2],
                                          scalar2=st1[:, 4:5], op0=ALU.add, op1=ALU.mult),
         reads=[pre.res, st1.res], writes=[xo.res])
    S.op("dve", lambda e: e.tensor_mul(out=xo[:, :], in0=xo[:, :], in1=gt[:, :]),
         reads=[xo.res, gt.res], writes=[xo.res])
    S.op("dve", lambda e: e.tensor_add(out=xo[:, :], in0=xo[:, :], in1=bt[:, :]),
         reads=[xo.res, bt.res], writes=[xo.res])
    xb = bufs["xb"].next()
    S.op("act", lambda e: e.activation(out=xb[:, :], in_=xo[:, :], func=AF.Copy),
         reads=[xo.res], writes=[xb.res])
    r0 = tok0
    S.dma("pool", lambda e: e.dma_start(out=k.dr["X"][r0:r0 + 128, :], in_=xo[:, :]),
          reads=[xo.res], writes=[k.dres["X"]])
    S.dma("pool", lambda e: e.dma_start(out=k.dr["XB"][r0:r0 + 128, :], in_=xb[:, :]),
          reads=[xb.res], writes=[k.dres["XB"]])
    pt = bufs["ptr"].next()
    for cc in range(8):
        S.op("pe", lambda e, cc=cc: e.transpose(out=pt[:, cc * 128:(cc + 1) * 128],
                                                 in_=xb[:, cc * 128:(cc + 1) * 128],
                                                 identity=C["ident_bf"][:, :]),
             reads=[xb.res, C["ident_bf"].res], writes=[pt.res])
    xt = bufs["xt"].next()
    S.op("act", lambda e: e.activation(out=xt[:, :], in_=pt[:, :], func=AF.Copy),
         reads=[pt.res], writes=[xt.res])
    col = xt_col(cfg, g) + lt
    XT = k.dr["XT"]
    S.dma("pool", lambda e: e.dma_start(
        out=XT[:, col:col + 128].rearrange("(c p) t -> p c t", p=128),
        in_=xt[:, :].rearrange("p (c t) -> p c t", c=8)),
        reads=[xt.res], writes=[k.dres["XT"]])


def ln_bufs(k, st):
    return dict(stat=k.sbring(st, 3, [128, 8], F32, "stat"),
                junk=k.sbring(st, 1, [128, 1024], F32, "junk"),
                xo=k.sbring(st, 2, [128, 1024], F32, "xo"),
                xb=k.sbring(st, 2, [128, 1024], BF16, "xb"),
                xt=k.sbring(st, 2, [128, 1024], BF16, "xt"),
                ptr=Ring([k.ps(st, [128, 1024], BF16, "ptr") for _ in range(2)]))


def load_ln_params(k, st, layer, which):
    S = k.S
    gt = k.sb(st, [128, 1024], F32, "lng")
    bt = k.sb(st, [128, 1024], F32, "lnb")
    S.dma("sp", lambda e: e.dma_start(out=gt[:, :], in_=k.dr["ln_g_rep"][layer, which, :, :]),
          reads=[k.dres["ln_g_rep"]], writes=[gt.res])
    S.dma("sp", lambda e: e.dma_start(out=bt[:, :], in_=k.dr["ln_b_rep"][layer, which, :, :]),
          reads=[k.dres["ln_b_rep"]], writes=[bt.res])
    return gt, bt


def load_w_bf16(k, st, wap, rows, cols, name, stage_ring):
    S = k.S
    nch = rows // 128
    wt = k.sb(st, [128, nch, cols], BF16, name)
    for cc in range(nch):
        sg = stage_ring.next()
        S.dma("sp", lambda e, cc=cc, sg=sg: e.dma_start(out=sg[:, 0:cols], in_=wap[cc * 128:(cc + 1) * 128, :]),
              reads=[], writes=[sg.res])
        eng = "act" if cc % 2 == 0 else "dve"
        if eng == "act":
            S.op("act", lambda e, cc=cc, sg=sg: e.activation(out=wt[:, cc, :], in_=sg[:, 0:cols], func=AF.Copy),
                 reads=[sg.res], writes=[wt.res])
        else:
            S.op("dve", lambda e, cc=cc, sg=sg: e.tensor_copy(out=wt[:, cc, :], in_=sg[:, 0:cols]),
                 reads=[sg.res], writes=[wt.res])
    return wt


def phase_init_x(k, C):
    S, cfg = k.S, k.cfg
    with ExitStack() as st:
        xin = k.sbring(st, 2, [128, 1024], F32, "xin")
        xbr = k.sbring(st, 2, [128, 1024], BF16, "xb0")
        xtr = k.sbring(st, 2, [128, 1024], BF16, "xt0")
        ptr = Ring([k.ps(st, [128, 1024], BF16, "ptr0") for _ in range(2)])
        for g in range(2):
            o, L = cfg.groups[g]
            for i in range(L // 128):
                r0 = o + i * 128
                xi = xin.next()
                S.dma("sp", lambda e, xi=xi, r0=r0: e.dma_start(out=xi[:, :], in_=k.dr["xin"][r0:r0 + 128, :]),
                      reads=[k.dres["xin"]], writes=[xi.res])
                S.dma("pool", lambda e, xi=xi, r0=r0: e.dma_start(out=k.dr["X"][r0:r0 + 128, :], in_=xi[:, :]),
                      reads=[xi.res], writes=[k.dres["X"]])
                xb = xbr.next()
                S.op("act", lambda e, xi=xi, xb=xb: e.activation(out=xb[:, :], in_=xi[:, :], func=AF.Copy),
                     reads=[xi.res], writes=[xb.res])
                S.dma("pool", lambda e, xb=xb, r0=r0: e.dma_start(out=k.dr["XB"][r0:r0 + 128, :], in_=xb[:, :]),
                      reads=[xb.res], writes=[k.dres["XB"]])
                pt = ptr.next()
                for cc in range(8):
                    S.op("pe", lambda e, cc=cc, xb=xb, pt=pt: e.transpose(
                        out=pt[:, cc * 128:(cc + 1) * 128], in_=xb[:, cc * 128:(cc + 1) * 128],
                        identity=C["ident_bf"][:, :]), reads=[xb.res, C["ident_bf"].res], writes=[pt.res])
                xt = xtr.next()
                S.op("dve", lambda e, xt=xt, pt=pt: e.tensor_copy(out=xt[:, :], in_=pt[:, :]),
                     reads=[pt.res], writes=[xt.res])
                col = xt_col(cfg, g) + i * 128
                S.dma("pool", lambda e, xt=xt, col=col: e.dma_start(
                    out=k.dr["XT"][:, col:col + 128].rearrange("(c p) t -> p c t", p=128),
                    in_=xt[:, :].rearrange("p (c t) -> p c t", c=8)),
                    reads=[xt.res], writes=[k.dres["XT"]])
    S.barrier()


def phase_proj_ln(k, C, src, wap, layer):
    S, cfg = k.S, k.cfg
    with ExitStack() as st:
        stage = k.sbring(st, 2, [128, 1024], F32, "wstage")
        wt = load_w_bf16(k, st, wap, 1024, 1024, "wout", stage)
        gt, bt = load_ln_params(k, st, layer, 0)
        bufs = ln_bufs(k, st)
        mring = k.sbring(st, 2, [128, 1024], BF16, "m")
        mtring = k.sbring(st, 2, [128, 1024], BF16, "mt")
        xres = k.sbring(st, 2, [128, 1024], F32, "xres")
        pre = k.sbring(st, 2, [128, 1024], F32, "pre")
        pmm = Ring([k.ps(st, [128, 512], F32, "pmm") for _ in range(4)])
        for g in range(2):
            o, L = cfg.groups[g]
            for i in range(L // 128):
                r0 = o + i * 128
                m = mring.next()
                S.dma("sp", lambda e, m=m, r0=r0: e.dma_start(out=m[:, :], in_=k.dr[src][r0:r0 + 128, :]),
                      reads=[k.dres[src]], writes=[m.res])
                xr = xres.next()
                S.dma("sp", lambda e, xr=xr, r0=r0: e.dma_start(out=xr[:, :], in_=k.dr["X"][r0:r0 + 128, :]),
                      reads=[k.dres["X"]], writes=[xr.res])
                pt = bufs["ptr"].next()
                for cc in range(8):
                    S.op("pe", lambda e, cc=cc, m=m, pt=pt: e.transpose(
                        out=pt[:, cc * 128:(cc + 1) * 128], in_=m[:, cc * 128:(cc + 1) * 128],
                        identity=C["ident_bf"][:, :]), reads=[m.res, C["ident_bf"].res], writes=[pt.res])
                mt = mtring.next()
                S.op("dve", lambda e, mt=mt, pt=pt: e.tensor_copy(out=mt[:, :], in_=pt[:, :]),
                     reads=[pt.res], writes=[mt.res])
                pr = pre.next()
                for h in range(2):
                    pm = pmm.next()
                    for cc in range(8):
                        S.op("pe", lambda e, cc=cc, h=h, pm=pm, mt=mt: e.matmul(
                            out=pm[:, :], lhsT=mt[:, cc * 128:(cc + 1) * 128],
                            rhs=wt[:, cc, h * 512:(h + 1) * 512], start=(cc == 0), stop=(cc == 7)),
                            reads=[mt.res, wt.res], writes=[pm.res])
                    S.op("dve", lambda e, h=h, pm=pm, pr=pr, xr=xr: e.scalar_tensor_tensor(
                        out=pr[:, h * 512:(h + 1) * 512], in0=xr[:, h * 512:(h + 1) * 512],
                        scalar=cfg.DN_ALPHA, in1=pm[:, :], op0=ALU.mult, op1=ALU.add),
                        reads=[pm.res, xr.res], writes=[pr.res])
                ln_store(k, C, bufs, pr, r0, g, i * 128, gt, bt)
    S.barrier()


def declare(k):
    cfg = k.cfg
    T, DEPTH = cfg.T, cfg.DEPTH
    NEV, NOD = (DEPTH + 1) // 2, max(1, DEPTH // 2)
    I = "ExternalInput"
    k.dram("xin", [T, 1024], F32, I)
    k.dram("svalid", [128, 1], F32, I)
    k.dram("ident_bf", [128, 128], BF16, I)
    k.dram("ones_bf", [128, 128], BF16, I)
    k.dram("ones_f", [128, 128], F32, I)
    k.dram("tri_bf", [128, 128], BF16, I)
    k.dram("ln_g_rep", [DEPTH, 2, 128, 1024], F32, I)
    k.dram("ln_b_rep", [DEPTH, 2, 128, 1024], F32, I)
    k.dram("ab_w_in", [NEV, 1024, 3072], F32, I)
    k.dram("ab_w_out", [NEV, 1024, 1024], F32, I)
    k.dram("c_w_out", [NOD, 1024, 1024], F32, I)
    k.dram("ec_router", [DEPTH, 1024, 16], F32, I)
    k.dram("ec_w_gate", [DEPTH, 2, 1024, cfg.DFF], F32, I)
    k.dram("ec_w_up", [DEPTH, 2, 1024, cfg.DFF], F32, I)
    k.dram("ec_w_down", [DEPTH, 2, cfg.DFF, 1024], F32, I)
    k.dram("hy_conv_w_rep", [NEV, 3, 128, 1536], F32, I)
    k.dram("hy_conv_b", [NEV, 1, 1536], F32, I)
    k.dram("rel_bias_rep", [128, 128], F32, I)
    k.dram("diff_lambda_rep", [NEV, 128, 4, 64], F32, I)
    k.dram("diff_subln_g_rep", [NEV, 128, 128], F32, I)
    for g in range(2):
        QC = min(512, cfg.groups[g][1])
        k.dram("bkt%d" % g, [128, QC // 128 + 2, QC], F32, I)
    k.dram("cdft", [256, 512], BF16, I)
    for g in range(2):
        N1f = cfg.groups[g][1] // 128
        k.dram("ffA%d" % g, [N1f, 2 * N1f], BF16, I)
        k.dram("ffB%d" % g, [N1f, 2 * N1f], BF16, I)
        k.dram("ffH%d" % g, [128, N1f, 3, 128], BF16, I)
    k.dram("esel", [128, 2, 16], F32, I)
    k.dram("eoff", [128, 16], F32, I)
    k.dram("mpre", [128, 128], F32, I)
    CAPTOT = cfg.CAP_P + cfg.CAP_S
    k.dram("X", [T, 1024], F32)
    k.dram("XB", [T, 1024], BF16)
    k.dram("XT", [1024, T + 4], BF16)
    dbgm = "ExternalOutput" if getattr(cfg, "debug", False) else "Internal"
    k.dram("MIX", [T, 1024], BF16, dbgm)
    k.dram("YC", [2, T, 1024], BF16)
    k.dram("QT", [512, T], BF16)
    k.dram("KT", [512, T], BF16)
    k.dram("V", [T, 512], BF16)
    k.dram("U", [3, T, 512], BF16, dbgm)
    k.dram("AFF", [T, 16], F32)
    k.dram("AFFALL", [8 * T, 16], F32)
    k.dram("XBALL", [8 * T, 1024], BF16)
    dbg = "ExternalOutput" if getattr(cfg, "debug", False) else "Internal"
    k.dram("MYPOS", [8 * T, 2], I32, dbg)
    k.dram("POSOWN", [T, 16], I32, dbg)
    k.dram("AFFMOWN", [T, 16], F32, dbg)
    dbg0 = "ExternalOutput" if getattr(cfg, "debug", False) else "Internal"
    k.dram("XSEL0", [CAPTOT, 1024], BF16, dbg0)
    k.dram("XSEL1", [CAPTOT, 1024], BF16, dbg0)
    for j in range(2):
        k.dram("YSEL%d" % j, [CAPTOT, 1024], BF16)
        k.dram("YALL%d" % j, [8 * CAPTOT, 1024], BF16)
    declare_hyena(k)
    k.dram("yP", [cfg.LP, 1024], F32, "ExternalOutput")
    k.dram("yS", [cfg.LS, 1024], F32, "ExternalOutput")


def host_consts(cfg):
    c = {}
    c["ident_bf"] = np.eye(128, dtype=np.float32).astype(NPBF)
    c["ones_bf"] = np.ones((128, 128), np.float32).astype(NPBF)
    c["ones_f"] = np.ones((128, 128), np.float32)
    c["tri_bf"] = np.triu(np.ones((128, 128), np.float32), 1).astype(NPBF)
    cc_, mm_ = np.arange(256)[:, None], np.arange(256)[None, :]
    ang = -2 * np.pi * (cc_ * mm_ % 256) / 256
    c["cdft"] = (np.concatenate([np.cos(ang), np.sin(ang)], 1) / 16.0).astype(np.float32).astype(NPBF)
    for g in range(2):
        L = cfg.groups[g][1]
        A, B, H = fft_const_arrays(L // 128, L // 128, False, scale=L ** -0.5)
        c["ffA%d" % g], c["ffB%d" % g], c["ffH%d" % g] = A, B, H
    c.update(hyena_host_consts(cfg))
    for g in range(2):
        QC = min(512, cfg.groups[g][1])
        nvar = QC // 128 + 2
        kk = np.arange(128)[:, None, None]
        v = np.arange(nvar)[None, :, None]
        qq = np.arange(QC)[None, None, :]
        c["bkt%d" % g] = rel_bucket_np(128 * (v - 1) + kk - qq).astype(np.float32)
    return c


def core_consts(cfg, r):
    CAPTOT = cfg.CAP_P + cfg.CAP_S
    esel = np.zeros((128, 2, 16), np.float32)
    esel[:, 0, 2 * r] = 1.0
    esel[:, 1, 2 * r + 1] = 1.0
    eoff = np.broadcast_to(((np.arange(16) // 2) * CAPTOT).astype(np.float32)[None, :], (128, 16)).copy()
    mpre = np.zeros((128, 128), np.float32)
    mpre[:16 * r, :] = 1.0
    return dict(esel=esel, eoff=eoff, mpre=mpre)


def rep128(a):
    a = np.asarray(a)
    return np.ascontiguousarray(np.broadcast_to(a[..., None, :], a.shape[:-1] + (128, a.shape[-1])))


BIGIDX = 500000.0


def allgather(k, src, dst):
    S = k.S
    S.cc(lambda e: e.collective_compute(
        "AllGather", ALU.bypass, replica_groups=[list(range(8))],
        ins=[k.dr[src].ap().opt()], outs=[k.dr[dst].ap().opt()]),
        reads=[k.dres[src]], writes=[k.dres[dst]])


def phase_moe(k, C, layer, final=False):
    S, cfg = k.S, k.cfg
    T = cfg.T
    CAPTOT = cfg.CAP_P + cfg.CAP_S
    caps = [cfg.CAP_P, cfg.CAP_S]
    slot0 = [0, cfg.CAP_P]
    S.skip = "router" in S.skipset
    with ExitStack() as st:
        stage = k.sbring(st, 2, [128, 1024], F32, "rstage")
        wr = load_w_bf16(k, st, k.dr["ec_router"][layer], 1024, 16, "wr", stage)
        xtr = k.sbring(st, 2, [128, 8, 128], BF16, "rxt")
        pr = Ring([k.ps(st, [128, 16], F32, "rps") for _ in range(2)])
        sm = k.sbring(st, 3, [128, 40], F32, "rsm")
        for g in range(2):
            o, L = cfg.groups[g]
            for i in range(L // 128):
                xt = xtr.next()
                col = xt_col(cfg, g) + i * 128
                S.dma("sp", lambda e, xt=xt, col=col: e.dma_start(
                    out=xt[:, :, :], in_=k.dr["XT"][:, col:col + 128].rearrange("(c p) t -> p c t", p=128)),
                    reads=[k.dres["XT"]], writes=[xt.res])
                ps = pr.next()
                for cc in range(8):
                    S.op("pe", lambda e, cc=cc, xt=xt, ps=ps: e.matmul(
                        out=ps[:, :], lhsT=xt[:, cc, :], rhs=wr[:, cc, :], start=(cc == 0), stop=(cc == 7)),
                        reads=[xt.res, wr.res], writes=[ps.res])
                s = sm.next()
                S.op("dve", lambda e, s=s, ps=ps: e.reduce_max(out=s[:, 0:1], in_=ps[:, :], axis=AX.X),
                     reads=[ps.res], writes=[s.res])
                S.op("dve", lambda e, s=s: e.tensor_scalar_mul(out=s[:, 1:2], in0=s[:, 0:1], scalar1=-1.0),
                     reads=[s.res], writes=[s.res])
                S.op("act", lambda e, s=s, ps=ps: e.activation(out=s[:, 8:24], in_=ps[:, :], func=AF.Exp,
                                                               bias=s[:, 1:2], scale=1.0, accum_out=s[:, 2:3]),
                     reads=[ps.res, s.res], writes=[s.res])
                S.op("dve", lambda e, s=s: e.reciprocal(out=s[:, 3:4], in_=s[:, 2:3]), reads=[s.res], writes=[s.res])
                if g == 1:
                    S.op("dve", lambda e, s=s: e.tensor_mul(out=s[:, 3:4], in0=s[:, 3:4], in1=C["svalid"][:, 0:1]),
                         reads=[s.res, C["svalid"].res], writes=[s.res])
                S.op("dve", lambda e, s=s: e.tensor_scalar_mul(out=s[:, 24:40], in0=s[:, 8:24], scalar1=s[:, 3:4]),
                     reads=[s.res], writes=[s.res])
                r0 = o + i * 128
                S.dma("pool", lambda e, s=s, r0=r0: e.dma_start(out=k.dr["AFF"][r0:r0 + 128, :], in_=s[:, 24:40]),
                      reads=[s.res], writes=[k.dres["AFF"]])
    S.barrier()
    S.skip = "ag1" in S.skipset
    allgather(k, "AFF", "AFFALL")
    allgather(k, "XB", "XBALL")
    S.barrier()
    S.skip = "thr" in S.skipset
    with ExitStack() as st:
        thr = [k.sb(st, [128, 16], F32, "thr") for _ in range(2)]
        base = [k.sb(st, [128, 16], F32, "base") for _ in range(2)]
        mpre = k.sb(st, [128, 128], F32, "mpre")
        S.dma("sp", lambda e: e.dma_start(out=mpre[:, :], in_=k.dr["mpre"][:, :]),
              reads=[k.dres["mpre"]], writes=[mpre.res])
        AFA = k.dr["AFFALL"]
        with ExitStack() as st2:
            for g in range(2):
                o, L = cfg.groups[g]
                nper = L // 16
                G = k.sb(st2, [128, nper, 16], F32, "G")
                junk = k.sb(st2, [128, nper, 16], F32, "Gj")
                for r in range(8):
                    S.dma("sp", lambda e, r=r, G=G: e.dma_start(
                        out=G[r * 16:(r + 1) * 16, :, :],
                        in_=AFA[r * T + o:r * T + o + L, :].rearrange("(a b) e -> a b e", a=16)),
                        reads=[k.dres["AFFALL"]], writes=[G.res])
                w = k.sb(st2, [128, 8, 16], F32, "bis")
                S.op("dve", lambda e, w=w: e.memset(w[:, 0, :], 0.0), writes=[w.res])
                S.op("dve", lambda e, w=w: e.memset(w[:, 1, :], 2.0), writes=[w.res])
                pc = k.ps(st2, [128, 16], F32, "pcnt")
                for it in range(34):
                    S.op("dve", lambda e, w=w: e.tensor_add(out=w[:, 2, :], in0=w[:, 0, :], in1=w[:, 1, :]),
                         reads=[w.res], writes=[w.res])
                    S.op("dve", lambda e, w=w: e.tensor_scalar_mul(out=w[:, 2, :], in0=w[:, 2, :], scalar1=0.5),
                         reads=[w.res], writes=[w.res])
                    S.op("dve", lambda e, w=w, G=G, junk=junk, nper=nper: e.tensor_tensor(
                        out=junk[:, :, :], in0=G[:, :, :],
                        in1=w[:, 2, :].unsqueeze(1).broadcast_to([128, nper, 16]), op=ALU.is_ge),
                        reads=[G.res, w.res], writes=[junk.res])
                    S.op("dve", lambda e, w=w, junk=junk: e.tensor_reduce(
                        out=w[:, 3, :], in_=junk[:, :, :].rearrange("p n e -> p e n"), axis=AX.X, op=ALU.add),
                        reads=[junk.res], writes=[w.res])
                    S.op("pe", lambda e, w=w, pc=pc: e.matmul(out=pc[:, :], lhsT=C["ones_f"][:, :], rhs=w[:, 3, :],
                                                             start=True, stop=True),
                         reads=[w.res, C["ones_f"].res], writes=[pc.res])
                    S.op("dve", lambda e, w=w, pc=pc, g=g: e.tensor_single_scalar(
                        out=w[:, 4, :], in_=pc[:, :], scalar=caps[g] - 0.5, op=ALU.is_ge),
                        reads=[pc.res], writes=[w.res])
                    S.op("dve", lambda e, w=w: e.tensor_scalar(out=w[:, 5, :], in0=w[:, 4, :], scalar1=-1.0, scalar2=1.0,
                                                               op0=ALU.mult, op1=ALU.add), reads=[w.res], writes=[w.res])
                    S.op("dve", lambda e, w=w: e.tensor_mul(out=w[:, 6, :], in0=w[:, 4, :], in1=w[:, 2, :]), reads=[w.res], writes=[w.res])
                    S.op("dve", lambda e, w=w: e.tensor_mul(out=w[:, 7, :], in0=w[:, 5, :], in1=w[:, 0, :]), reads=[w.res], writes=[w.res])
                    S.op("dve", lambda e, w=w: e.tensor_add(out=w[:, 0, :], in0=w[:, 6, :], in1=w[:, 7, :]), reads=[w.res], writes=[w.res])
                    S.op("dve", lambda e, w=w: e.tensor_mul(out=w[:, 6, :], in0=w[:, 5, :], in1=w[:, 2, :]), reads=[w.res], writes=[w.res])
                    S.op("dve", lambda e, w=w: e.tensor_mul(out=w[:, 7, :], in0=w[:, 4, :], in1=w[:, 1, :]), reads=[w.res], writes=[w.res])
                    S.op("dve", lambda e, w=w: e.tensor_add(out=w[:, 1, :], in0=w[:, 6, :], in1=w[:, 7, :]), reads=[w.res], writes=[w.res])
                S.op("dve", lambda e, w=w, g=g: e.tensor_copy(out=thr[g][:, :], in_=w[:, 0, :]),
                     reads=[w.res], writes=[thr[g].res])
                S.op("dve", lambda e, w=w, G=G, junk=junk, nper=nper: e.tensor_tensor(
                    out=junk[:, :, :], in0=G[:, :, :],
                    in1=w[:, 0, :].unsqueeze(1).broadcast_to([128, nper, 16]), op=ALU.is_ge),
                    reads=[G.res, w.res], writes=[junk.res])
                S.op("dve", lambda e, w=w, junk=junk: e.tensor_reduce(
                    out=w[:, 3, :], in_=junk[:, :, :].rearrange("p n e -> p e n"), axis=AX.X, op=ALU.add),
                    reads=[junk.res], writes=[w.res])
                S.op("pe", lambda e, w=w, pc=pc: e.matmul(out=pc[:, :], lhsT=mpre[:, :], rhs=w[:, 3, :],
                                                         start=True, stop=True),
                     reads=[w.res, mpre.res], writes=[pc.res])
                S.op("dve", lambda e, pc=pc, g=g: e.tensor_copy(out=base[g][:, :], in_=pc[:, :]),
                     reads=[pc.res], writes=[base[g].res])
                S.barrier()
        S.skip = "pos" in S.skipset
        NB = 8
        with ExitStack() as st2:
            sel = k.sb(st2, [128, 2, 16], F32, "sel")
            S.dma("sp", lambda e: e.dma_start(out=sel[:, :, :], in_=k.dr["esel"][:, :, :]),
                  reads=[k.dres["esel"]], writes=[sel.res])
            eoff = k.sb(st2, [128, 16], F32, "eoff")
            S.dma("sp", lambda e: e.dma_start(out=eoff[:, :], in_=k.dr["eoff"][:, :]),
                  reads=[k.dres["eoff"]], writes=[eoff.res])
            affr = k.sbring(st2, 2, [128, NB, 16], F32, "affr")
            wk = k.sbring(st2, 2, [128, 8, NB, 16], F32, "pwk")
            mb = k.sbring(st2, 2, [128, NB, 16], BF16, "mb")
            run = k.sb(st2, [128, 16], F32, "run")
            ppre = Ring([k.ps(st2, [128, NB * 16], F32, "ppre") for _ in range(2)])
            ptot = Ring([k.ps(st2, [128, NB * 16], F32, "ptot") for _ in range(2)])
            posi = k.sbring(st2, 2, [128, NB, 16], I32, "posi")
            myp = k.sbring(st2, 2, [128, NB, 2], F32, "myp")
            mypi = k.sbring(st2, 2, [128, NB, 2], I32, "mypi")
            def batch(srcn, row0, nb, g, own):
                a = affr.next()
                S.dma("sp", lambda e, a=a, row0=row0, nb=nb: e.dma_start(
                    out=a[:, 0:nb, :], in_=k.dr[srcn][row0:row0 + nb * 128, :].rearrange("(j p) e -> p j e", p=128)),
                    reads=[k.dres[srcn]], writes=[a.res])
                w = wk.next()
                m = mb.next()
                S.op("dve", lambda e, a=a, m=m, nb=nb, g=g: e.tensor_tensor(
                    out=m[:, 0:nb, :], in0=a[:, 0:nb, :],
                    in1=thr[g][:, :].unsqueeze(1).broadcast_to([128, nb, 16]), op=ALU.is_ge),
                    reads=[a.res, thr[g].res], writes=[m.res])
                pp, pt = ppre.next(), ptot.next()
                S.op("pe", lambda e, m=m, pp=pp, nb=nb: e.matmul(
                    out=pp[:, 0:nb * 16], lhsT=C["tri_bf"][:, :], rhs=m[:, 0:nb, :].rearrange("p j e -> p (j e)"),
                    start=True, stop=True), reads=[m.res, C["tri_bf"].res], writes=[pp.res])
                S.op("pe", lambda e, m=m, pt=pt, nb=nb: e.matmul(
                    out=pt[:, 0:nb * 16], lhsT=C["ones_bf"][:, :], rhs=m[:, 0:nb, :].rearrange("p j e -> p (j e)"),
                    start=True, stop=True), reads=[m.res, C["ones_bf"].res], writes=[pt.res])
                for j in range(nb):
                    S.op("dve", lambda e, w=w, pp=pp, j=j: e.tensor_add(
                        out=w[:, 0, j, :], in0=pp[:, j * 16:(j + 1) * 16], in1=run[:, :]),
                        reads=[pp.res, run.res], writes=[w.res])
                    S.op("dve", lambda e, pt=pt, j=j: e.tensor_add(
                        out=run[:, :], in0=run[:, :], in1=pt[:, j * 16:(j + 1) * 16]),
                        reads=[pt.res, run.res], writes=[run.res])
                S.op("dve", lambda e, w=w, nb=nb, g=g: e.tensor_single_scalar(
                    out=w[:, 1, 0:nb, :], in_=w[:, 0, 0:nb, :], scalar=caps[g] - 0.5, op=ALU.is_lt),
                    reads=[w.res], writes=[w.res])
                S.op("dve", lambda e, w=w, m=m, nb=nb: e.tensor_mul(
                    out=w[:, 1, 0:nb, :], in0=w[:, 1, 0:nb, :], in1=m[:, 0:nb, :]),
                    reads=[w.res, m.res], writes=[w.res])
                S.op("dve", lambda e, w=w, a=a, nb=nb: e.tensor_mul(
                    out=w[:, 2, 0:nb, :], in0=w[:, 1, 0:nb, :], in1=a[:, 0:nb, :]),
                    reads=[w.res, a.res], writes=[w.res])
                S.op("dve", lambda e, w=w, nb=nb, g=g: e.tensor_scalar_add(
                    out=w[:, 3, 0:nb, :], in0=w[:, 0, 0:nb, :], scalar1=float(slot0[g]) - BIGIDX),
                    reads=[w.res], writes=[w.res])
                S.op("dve", lambda e, w=w, nb=nb: e.tensor_mul(
                    out=w[:, 3, 0:nb, :], in0=w[:, 3, 0:nb, :], in1=w[:, 1, 0:nb, :]),
                    reads=[w.res], writes=[w.res])
                S.op("dve", lambda e, w=w, nb=nb: e.tensor_scalar_add(
                    out=w[:, 3, 0:nb, :], in0=w[:, 3, 0:nb, :], scalar1=BIGIDX),
                    reads=[w.res], writes=[w.res])
                if not own:
                    mp = myp.next()
                    for jj in range(2):
                        S.op("dve", lambda e, w=w, nb=nb, jj=jj: e.tensor_tensor(
                            out=w[:, 4, 0:nb, :], in0=w[:, 3, 0:nb, :],
                            in1=sel[:, jj, :].unsqueeze(1).broadcast_to([128, nb, 16]), op=ALU.mult),
                            reads=[w.res, sel.res], writes=[w.res])
                        S.op("dve", lambda e, w=w, mp=mp, nb=nb, jj=jj: e.tensor_reduce(
                            out=mp[:, 0:nb, jj], in_=w[:, 4, 0:nb, :], axis=AX.X, op=ALU.add),
                            reads=[w.res], writes=[mp.res])
                    mi = mypi.next()
                    S.op("dve", lambda e, mi=mi, mp=mp, nb=nb: e.tensor_copy(out=mi[:, 0:nb, :], in_=mp[:, 0:nb, :]),
                         reads=[mp.res], writes=[mi.res])
                    S.dma("pool", lambda e, mi=mi, row0=row0, nb=nb: e.dma_start(
                        out=k.dr["MYPOS"][row0:row0 + nb * 128, :].rearrange("(j p) e -> p j e", p=128),
                        in_=mi[:, 0:nb, :]), reads=[mi.res], writes=[k.dres["MYPOS"]])
                else:
                    S.op("dve", lambda e, w=w, nb=nb, g=g: e.tensor_scalar_add(
                        out=w[:, 5, 0:nb, :], in0=w[:, 0, 0:nb, :], scalar1=float(slot0[g])),
                        reads=[w.res], writes=[w.res])
                    S.op("dve", lambda e, w=w, nb=nb: e.tensor_tensor(
                        out=w[:, 5, 0:nb, :], in0=w[:, 5, 0:nb, :],
                        in1=eoff[:, :].unsqueeze(1).broadcast_to([128, nb, 16]), op=ALU.add),
                        reads=[w.res, eoff.res], writes=[w.res])
                    S.op("dve", lambda e, w=w, nb=nb: e.tensor_mul(
                        out=w[:, 5, 0:nb, :], in0=w[:, 5, 0:nb, :], in1=w[:, 1, 0:nb, :]),
                        reads=[w.res], writes=[w.res])
                    pi = posi.next()
                    S.op("dve", lambda e, w=w, pi=pi, nb=nb: e.tensor_copy(out=pi[:, 0:nb, :], in_=w[:, 5, 0:nb, :]),
                         reads=[w.res], writes=[pi.res])
                    S.dma("pool", lambda e, pi=pi, row0=row0, nb=nb: e.dma_start(
                        out=k.dr["POSOWN"][row0:row0 + nb * 128, :].rearrange("(j p) e -> p j e", p=128),
                        in_=pi[:, 0:nb, :]), reads=[pi.res], writes=[k.dres["POSOWN"]])
                    S.dma("pool", lambda e, w=w, row0=row0, nb=nb: e.dma_start(
                        out=k.dr["AFFMOWN"][row0:row0 + nb * 128, :].rearrange("(j p) e -> p j e", p=128),
                        in_=w[:, 2, 0:nb, :]), reads=[w.res], writes=[k.dres["AFFMOWN"]])

            for g in range(2):
                o, L = cfg.groups[g]
                S.op("dve", lambda e: e.memset(run[:, :], 0.0), writes=[run.res])
                nt = L // 128
                for r in range(8):
                    for b0 in range(0, nt, NB):
                        batch("AFFALL", r * T + o + b0 * 128, min(NB, nt - b0), g, False)
            for g in range(2):
                o, L = cfg.groups[g]
                S.op("dve", lambda e, g=g: e.tensor_copy(out=run[:, :], in_=base[g][:, :]),
                     reads=[base[g].res], writes=[run.res])
                nt = L // 128
                for b0 in range(0, nt, NB):
                    batch("AFF", o + b0 * 128, min(NB, nt - b0), g, True)
    S.barrier()
    S.skip = "dispatch" in S.skipset
    with ExitStack() as st:
        xr = k.sbring(st, 3, [128, 1024], BF16, "dx")
        ir = k.sbring(st, 3, [128, 2], I32, "di")
        for g in range(2):
            o, L = cfg.groups[g]
            for r in range(8):
                for i in range(L // 128):
                    row0 = r * T + o + i * 128
                    x = xr.next()
                    ix = ir.next()
                    S.dma("sp", lambda e, x=x, row0=row0: e.dma_start(out=x[:, :], in_=k.dr["XBALL"][row0:row0 + 128, :]),
                          reads=[k.dres["XBALL"]], writes=[x.res])
                    S.dma("sp", lambda e, ix=ix, row0=row0: e.dma_start(out=ix[:, :], in_=k.dr["MYPOS"][row0:row0 + 128, :]),
                          reads=[k.dres["MYPOS"]], writes=[ix.res])
                    for jj in range(2):
                        S.dma("pool", lambda e, x=x, ix=ix, jj=jj: e.indirect_dma_start(
                            out=k.dr["XSEL%d" % jj][:, :], out_offset=bass.IndirectOffsetOnAxis(ap=ix[:, jj:jj + 1], axis=0),
                            in_=x[:, :], in_offset=None, bounds_check=k.bcreg(CAPTOT - 1), oob_is_err=False),
                            reads=[x.res, ix.res], writes=[k.dres["XSEL%d" % jj]])
    S.barrier()
    S.skip = "ffn" in S.skipset
    DFF = cfg.DFF
    NF = DFF // 128
    SB_ = 256
    for jj in range(2):
        with ExitStack() as st:
            stage = k.sbring(st, 2, [128, 1024], F32, "estage")
            wg = k.sb(st, [128, 8, DFF], BF16, "wg")
            wu = k.sb(st, [128, 8, DFF], BF16, "wu")
            wd = k.sb(st, [128, NF, 1024], BF16, "wd")
            ci = 0
            for (wt, name, nch, ncol) in ((wg, "ec_w_gate", 8, DFF), (wu, "ec_w_up", 8, DFF), (wd, "ec_w_down", NF, 1024)):
                for cc in range(nch):
                    for c0 in range(0, ncol, 1024):
                        cw = min(1024, ncol - c0)
                        sg = stage.next()
                        S.dma("sp", lambda e, sg=sg, name=name, cc=cc, c0=c0, cw=cw: e.dma_start(
                            out=sg[:, 0:cw], in_=k.dr[name][layer, jj, cc * 128:(cc + 1) * 128, c0:c0 + cw]),
                            reads=[], writes=[sg.res])
                        ci += 1
                        if ci % 2:
                            S.op("act", lambda e, sg=sg, wt=wt, cc=cc, c0=c0, cw=cw: e.activation(
                                out=wt[:, cc, c0:c0 + cw], in_=sg[:, 0:cw], func=AF.Copy), reads=[sg.res], writes=[wt.res])
                        else:
                            S.op("dve", lambda e, sg=sg, wt=wt, cc=cc, c0=c0, cw=cw: e.tensor_copy(
                                out=wt[:, cc, c0:c0 + cw], in_=sg[:, 0:cw]), reads=[sg.res], writes=[wt.res])
            xrow = k.sbring(st, 2, [128, 1024], BF16, "ex")
            xgt = k.sbring(st, 2, [128, 8, SB_], BF16, "exT")
            hT = k.sb(st, [128, NF, SB_], BF16, "hT")
            gs = k.sbring(st, 2, [128, SB_], F32, "gs")
            yo = k.sbring(st, 2, [128, 1024], BF16, "yo")
            ptr = Ring([k.ps(st, [128, 1024], BF16, "eptr") for _ in range(2)])
            pg = Ring([k.ps(st, [128, 512], F32, "epg") for _ in range(4)])
            for s0 in range(0, CAPTOT, SB_):
                nsl = min(SB_, CAPTOT - s0)
                xg = xgt.next()
                for ti in range(nsl // 128):
                    x = xrow.next()
                    S.dma("sp", lambda e, x=x, s0=s0, ti=ti: e.dma_start(
                        out=x[:, :], in_=k.dr["XSEL%d" % jj][s0 + ti * 128:s0 + (ti + 1) * 128, :]),
                        reads=[k.dres["XSEL%d" % jj]], writes=[x.res])
                    pt = ptr.next()
                    for cc in range(8):
                        S.op("pe", lambda e, cc=cc, x=x, pt=pt: e.transpose(
                            out=pt[:, cc * 128:(cc + 1) * 128], in_=x[:, cc * 128:(cc + 1) * 128],
                            identity=C["ident_bf"][:, :]), reads=[x.res, C["ident_bf"].res], writes=[pt.res])
                    S.op("dve", lambda e, xg=xg, pt=pt, ti=ti: e.tensor_copy(
                        out=xg[:, :, ti * 128:(ti + 1) * 128], in_=pt[:, :].rearrange("p (c t) -> p c t", c=8)),
                        reads=[pt.res], writes=[xg.res])
                for f in range(NF):
                    p1, p2 = pg.next(), pg.next()
                    for cc in range(8):
                        S.op("pe", lambda e, cc=cc, f=f, p1=p1, xg=xg, nsl=nsl: e.matmul(
                            out=p1[:, 0:nsl], lhsT=wg[:, cc, f * 128:(f + 1) * 128], rhs=xg[:, cc, 0:nsl],
                            start=(cc == 0), stop=(cc == 7)), reads=[wg.res, xg.res], writes=[p1.res])
                    for cc in range(8):
                        S.op("pe", lambda e, cc=cc, f=f, p2=p2, xg=xg, nsl=nsl: e.matmul(
                            out=p2[:, 0:nsl], lhsT=wu[:, cc, f * 128:(f + 1) * 128], rhs=xg[:, cc, 0:nsl],
                            start=(cc == 0), stop=(cc == 7)), reads=[wu.res, xg.res], writes=[p2.res])
                    gt_ = gs.next()
                    S.op("act", lambda e, gt_=gt_, p1=p1, nsl=nsl: e.activation(out=gt_[:, 0:nsl], in_=p1[:, 0:nsl], func=AF.Silu),
                         reads=[p1.res], writes=[gt_.res])
                    S.op("dve", lambda e, gt_=gt_, p2=p2, f=f, nsl=nsl: e.tensor_mul(
                        out=hT[:, f, 0:nsl], in0=gt_[:, 0:nsl], in1=p2[:, 0:nsl]),
                        reads=[gt_.res, p2.res], writes=[hT.res])
                for ti in range(nsl // 128):
                    y = yo.next()
                    for h in range(2):
                        pd = pg.next()
                        for f in range(NF):
                            S.op("pe", lambda e, f=f, h=h, pd=pd, ti=ti: e.matmul(
                                out=pd[:, :], lhsT=hT[:, f, ti * 128:(ti + 1) * 128], rhs=wd[:, f, h * 512:(h + 1) * 512],
                                start=(f == 0), stop=(f == NF - 1)), reads=[hT.res, wd.res], writes=[pd.res])
                        if h == 0:
                            S.op("act", lambda e, y=y, pd=pd: e.activation(out=y[:, 0:512], in_=pd[:, :], func=AF.Copy),
                                 reads=[pd.res], writes=[y.res])
                        else:
                            S.op("dve", lambda e, y=y, pd=pd: e.tensor_copy(out=y[:, 512:1024], in_=pd[:, :]),
                                 reads=[pd.res], writes=[y.res])
                    S.dma("pool", lambda e, y=y, s0=s0, ti=ti: e.dma_start(
                        out=k.dr["YSEL%d" % jj][s0 + ti * 128:s0 + (ti + 1) * 128, :], in_=y[:, :]),
                        reads=[y.res], writes=[k.dres["YSEL%d" % jj]])
        S.barrier()
    S.skip = "ag2" in S.skipset
    allgather(k, "YSEL0", "YALL0")
    allgather(k, "YSEL1", "YALL1")
    S.barrier()
    S.skip = "combine" in S.skipset
    with ExitStack() as st:
        gt, bt = load_ln_params(k, st, layer, 1)
        bufs = ln_bufs(k, st)
        xres = k.sbring(st, 2, [128, 1024], F32, "cx")
        acc = k.sbring(st, 2, [128, 1024], F32, "cacc")
        gat = k.sbring(st, 4, [128, 1024], BF16, "cgat")
        for t in gat.tiles:
            S.op("dve", lambda e, t=t: e.memset(t[:, :], 0.0), writes=[t.res])
        ir = k.sbring(st, 2, [128, 16], I32, "cidx")
        ar = k.sbring(st, 2, [128, 16], F32, "caff")
        for g in range(2):
            o, L = cfg.groups[g]
            for i in range(L // 128):
                r0 = o + i * 128
                xr_ = xres.next()
                S.dma("sp", lambda e, xr_=xr_, r0=r0: e.dma_start(out=xr_[:, :], in_=k.dr["X"][r0:r0 + 128, :]),
                      reads=[k.dres["X"]], writes=[xr_.res])
                ix, af = ir.next(), ar.next()
                S.dma("sp", lambda e, ix=ix, r0=r0: e.dma_start(out=ix[:, :], in_=k.dr["POSOWN"][r0:r0 + 128, :]),
                      reads=[k.dres["POSOWN"]], writes=[ix.res])
                S.dma("sp", lambda e, af=af, r0=r0: e.dma_start(out=af[:, :], in_=k.dr["AFFMOWN"][r0:r0 + 128, :]),
                      reads=[k.dres["AFFMOWN"]], writes=[af.res])
                a = acc.next()
                S.op("act", lambda e, a=a, xr_=xr_: e.activation(out=a[:, :], in_=xr_[:, :], func=AF.Copy, scale=cfg.DN_ALPHA),
                     reads=[xr_.res], writes=[a.res])
                for ex in range(16):
                    gt_ = gat.next()
                    S.dma("pool", lambda e, gt_=gt_, ix=ix, ex=ex: e.indirect_dma_start(
                        out=gt_[:, :], out_offset=None, in_=k.dr["YALL%d" % (ex % 2)][:, :],
                        in_offset=bass.IndirectOffsetOnAxis(ap=ix[:, ex:ex + 1], axis=0),
                        bounds_check=k.bcreg(8 * CAPTOT - 1), oob_is_err=False),
                        reads=[ix.res, k.dres["YALL%d" % (ex % 2)]], writes=[gt_.res])
                    S.op("dve", lambda e, a=a, gt_=gt_, af=af, ex=ex: e.scalar_tensor_tensor(
                        out=a[:, :], in0=gt_[:, :], scalar=af[:, ex:ex + 1], in1=a[:, :], op0=ALU.mult, op1=ALU.add),
                        reads=[gt_.res, af.res, a.res], writes=[a.res])
                ln_store(k, C, bufs, a, r0, g, i * 128, gt, bt)
    S.skip = False
    S.barrier()


def phase_mixer_even(k, C, j, layer):
    phase_inproj(k, C, j)
    phase_attn(k, C, j, layer)
    phase_hyena(k, C, j)


def phase_mixer_odd(k, C, j):
    phase_fnet(k, C, j)


def build_program(cfg):
    nc = bass.Bass("TRN2", target_bir_lowering=False)
    with ExitStack() as stack:
        k = K(nc, cfg, stack)
        declare(k)
        S = k.S
        with ExitStack() as st:
            C = load_consts(k, st)
            phase_init_x(k, C)
            for layer in range(cfg.DEPTH):
                j = layer // 2
                if layer % 2 == 0:
                    phase_mixer_even(k, C, j, layer)
                    phase_proj_ln(k, C, "MIX", k.dr["ab_w_out"][j], layer)
                else:
                    phase_mixer_odd(k, C, j)
                    phase_proj_ln(k, C, "MIX", k.dr["c_w_out"][j], layer)
                phase_moe(k, C, layer)
            t = k.sbring(st, 2, [128, 1024], F32, "o")
            for name, (o, L) in zip(("yP", "yS"), cfg.groups):
                for i in range(L // 128):
                    tt = t.next()
                    S.dma("sp", lambda e, tt=tt, r=o + i * 128: e.dma_start(out=tt[:, :], in_=k.dr["X"][r:r + 128, :]),
                          reads=[k.dres["X"]], writes=[tt.res])
                    S.dma("pool", lambda e, tt=tt, name=name, i=i: e.dma_start(
                        out=k.dr[name][i * 128:(i + 1) * 128, :], in_=tt[:, :]), reads=[tt.res], writes=[k.dres[name]])
            S.finish()
        build_program.last_ninstr = S.ninstr
    return nc


def kernel(**inputs):
    return _kernel_impl(Cfg(), **inputs)


def _kernel_impl(cfg, x_prompt, x_sample, rel_bias, ab_w_in, ab_w_out, diff_lambda, diff_subln_g, hy_conv_w, hy_conv_b,
                 hy_f_w1, hy_f_b1, hy_f_w2, hy_f_b2, hy_f_w3, hy_f_freq, hy_decay, hy_skip,
                 c_w_out, ec_router, ec_w_gate, ec_w_up, ec_w_down, ln_g, ln_b):
    nc = build_program(cfg)
    hc = host_consts(cfg)
    in_maps = []
    for r in range(8):
        im = dict(hc)
        im.update(core_consts(cfg, r))
        im["xin"] = np.concatenate([np.asarray(x_prompt[r]), np.asarray(x_sample[r % 4])], 0)
        im["svalid"] = np.full((128, 1), 1.0 if r < 4 else 0.0, np.float32)
        im["ln_g_rep"] = rep128(ln_g)
        im["ln_b_rep"] = rep128(ln_b)
        im["ab_w_in"] = np.asarray(ab_w_in)
        im["ab_w_out"] = np.asarray(ab_w_out)
        im["c_w_out"] = np.asarray(c_w_out)
        im["ec_router"] = np.asarray(ec_router)
        im["ec_w_gate"] = np.ascontiguousarray(ec_w_gate[:, 2 * r:2 * r + 2])
        im["ec_w_up"] = np.ascontiguousarray(ec_w_up[:, 2 * r:2 * r + 2])
        im["ec_w_down"] = np.ascontiguousarray(ec_w_down[:, 2 * r:2 * r + 2])
        im["hy_conv_w_rep"] = rep128(hy_conv_w)
        im["hy_conv_b"] = np.asarray(hy_conv_b)[:, None, :]
        im["rel_bias_rep"] = rep128(np.asarray(rel_bias).reshape(-1))
        im["diff_lambda_rep"] = np.ascontiguousarray(np.broadcast_to(np.asarray(diff_lambda)[:, None], (diff_lambda.shape[0], 128, 4, 64)))
        im["diff_subln_g_rep"] = rep128(diff_subln_g)
        im["hy_f_w1"] = np.asarray(hy_f_w1)
        im["hy_f_w2"] = np.asarray(hy_f_w2)
        im["hy_f_w3"] = np.asarray(hy_f_w3)
        im["hy_f_b1c"] = np.asarray(hy_f_b1)[:, :, None]
        im["hy_f_b2c"] = np.asarray(hy_f_b2)[:, :, None]
        im["hy_f_freqc"] = np.ascontiguousarray(np.transpose(np.asarray(hy_f_freq), (0, 2, 1)))
        im["hy_decay_rep"] = rep128(hy_decay)
        im["hy_skip_rep"] = rep128(hy_skip)
        in_maps.append(im)
    res = run_bass_kernel_spmd(nc, in_maps, core_ids=list(range(8)))
    y_prompt = np.stack([res.results[r]["yP"] for r in range(8)]).astype(np.float32)
    y_sample = np.stack([res.results[r]["yS"] for r in range(4)]).astype(np.float32)
    return (y_prompt, y_sample)


def phase_inproj(k, C, j):
    S, cfg = k.S, k.cfg
    with ExitStack() as st:
        stage = k.sbring(st, 2, [128, 3072], F32, "istage")
        wqkv = k.sb(st, [128, 8, 1536], BF16, "wqkv")
        wsc = [k.sb(st, [128, 8, 1536], BF16, "wsc") for _ in range(3)]
        cw = k.sb(st, [128, 3, 1536], F32, "cw")
        S.dma("sp", lambda e: e.dma_start(out=cw[:, :, :], in_=k.dr["hy_conv_w_rep"][j].rearrange("s p c -> p s c")),
              reads=[k.dres["hy_conv_w_rep"]], writes=[cw.res])
        cbf = k.sb(st, [1, 1536], F32, "cbf")
        cb = k.sb(st, [1, 1536], BF16, "cb")
        S.dma("sp", lambda e: e.dma_start(out=cbf[:, :], in_=k.dr["hy_conv_b"][j]),
              reads=[k.dres["hy_conv_b"]], writes=[cbf.res])
        S.op("dve", lambda e: e.tensor_copy(out=cb[:, :], in_=cbf[:, :]), reads=[cbf.res], writes=[cb.res])
        for cc in range(8):
            sg = stage.next()
            S.dma("sp", lambda e, sg=sg, cc=cc: e.dma_start(out=sg[:, :], in_=k.dr["ab_w_in"][j, cc * 128:(cc + 1) * 128, :]),
                  reads=[], writes=[sg.res])
            S.op("act", lambda e, sg=sg, cc=cc: e.activation(out=wqkv[:, cc, :], in_=sg[:, 0:1536], func=AF.Copy),
                 reads=[sg.res], writes=[wqkv.res])
            for s3 in range(3):
                S.op("dve", lambda e, sg=sg, cc=cc, s3=s3: e.tensor_mul(out=wsc[s3][:, cc, :], in0=sg[:, 1536:3072], in1=cw[:, s3, :]),
                     reads=[sg.res, cw.res], writes=[wsc[s3].res])
        xtr = k.sbring(st, 2, [128, 8, 514], BF16, "ixt")
        qkr = k.sbring(st, 3, [128, 512], BF16, "iqk")
        vr = k.sbring(st, 3, [128, 512], BF16, "iv")
        pp = Ring([k.ps(st, [128, 512], F32, "ipp") for _ in range(4)])
        ei = 0
        for g in range(2):
            o, L = cfg.groups[g]
            BL = min(512, L)
            for b in range(L // BL):
                xt = xtr.next()
                col = xt_col(cfg, g) + b * BL - 1
                S.dma("sp", lambda e, xt=xt, col=col, BL=BL: e.dma_start(
                    out=xt[:, :, 0:BL + 2], in_=k.dr["XT"][:, col:col + BL + 2].rearrange("(c p) t -> p c t", p=128)),
                    reads=[k.dres["XT"]], writes=[xt.res])
                t0 = o + b * BL
                for ch in range(8):
                    ps = pp.next()
                    for cc in range(8):
                        S.op("pe", lambda e, ps=ps, xt=xt, cc=cc, ch=ch, BL=BL: e.matmul(
                            out=ps[:, 0:BL], lhsT=wqkv[:, cc, ch * 128:(ch + 1) * 128], rhs=xt[:, cc, 1:BL + 1],
                            start=(cc == 0), stop=(cc == 7)), reads=[wqkv.res, xt.res], writes=[ps.res])
                    qk = qkr.next()
                    ei += 1
                    if ei % 2:
                        S.op("act", lambda e, qk=qk, ps=ps, BL=BL: e.activation(out=qk[:, 0:BL], in_=ps[:, 0:BL], func=AF.Copy),
                             reads=[ps.res], writes=[qk.res])
                    else:
                        S.op("dve", lambda e, qk=qk, ps=ps, BL=BL: e.tensor_copy(out=qk[:, 0:BL], in_=ps[:, 0:BL]),
                             reads=[ps.res], writes=[qk.res])
                    dn = "QT" if ch < 4 else "KT"
                    S.dma("pool", lambda e, qk=qk, dn=dn, ch=ch, t0=t0, BL=BL: e.dma_start(
                        out=k.dr[dn][(ch % 4) * 128:(ch % 4 + 1) * 128, t0:t0 + BL], in_=qk[:, 0:BL]),
                        reads=[qk.res], writes=[k.dres[dn]])
                for tj in range(BL // 128):
                    r0 = t0 + tj * 128
                    ps = pp.next()
                    for cc in range(8):
                        S.op("pe", lambda e, ps=ps, xt=xt, cc=cc, tj=tj: e.matmul(
                            out=ps[:, :], lhsT=xt[:, cc, 1 + tj * 128:1 + (tj + 1) * 128], rhs=wqkv[:, cc, 1024:1536],
                            start=(cc == 0), stop=(cc == 7)), reads=[wqkv.res, xt.res], writes=[ps.res])
                    v = vr.next()
                    S.op("act", lambda e, v=v, ps=ps: e.activation(out=v[:, :], in_=ps[:, :], func=AF.Copy),
                         reads=[ps.res], writes=[v.res])
                    S.dma("pool", lambda e, v=v, r0=r0: e.dma_start(out=k.dr["V"][r0:r0 + 128, :], in_=v[:, :]),
                          reads=[v.res], writes=[k.dres["V"]])
                    for uc in range(3):
                        ps = pp.next()
                        n = 0
                        for s3 in range(3):
                            for cc in range(8):
                                S.op("pe", lambda e, ps=ps, xt=xt, cc=cc, tj=tj, s3=s3, uc=uc, n=n: e.matmul(
                                    out=ps[:, :], lhsT=xt[:, cc, s3 + tj * 128:s3 + (tj + 1) * 128],
                                    rhs=wsc[s3][:, cc, uc * 512:(uc + 1) * 512], start=(n == 0), stop=False),
                                    reads=[wsc[s3].res, xt.res], writes=[ps.res])
                                n += 1
                        S.op("pe", lambda e, ps=ps, uc=uc: e.matmul(
                            out=ps[:, :], lhsT=C["ones_bf"][0:1, :], rhs=cb[0:1, uc * 512:(uc + 1) * 512],
                            start=False, stop=True), reads=[cb.res, C["ones_bf"].res], writes=[ps.res])
                        u = vr.next()
                        S.op("dve", lambda e, u=u, ps=ps: e.tensor_copy(out=u[:, :], in_=ps[:, :]),
                             reads=[ps.res], writes=[u.res])
                        S.dma("pool", lambda e, u=u, r0=r0, uc=uc: e.dma_start(
                            out=k.dr["U"][uc, r0:r0 + 128, :], in_=u[:, :]), reads=[u.res], writes=[k.dres["U"]])
    S.barrier()


def phase_attn(k, C, j, layer):
    S, cfg = k.S, k.cfg
    lam_init = 0.8 - 0.6 * math.exp(-0.3 * layer)
    with ExitStack() as st:
        tab = k.sb(st, [128, 128], F32, "tab")
        S.dma("sp", lambda e: e.dma_start(out=tab[:, :], in_=k.dr["rel_bias_rep"][:, :]),
              reads=[k.dres["rel_bias_rep"]], writes=[tab.res])
        lam = k.sb(st, [128, 4, 64], F32, "lam")
        S.dma("sp", lambda e: e.dma_start(out=lam[:, :, :], in_=k.dr["diff_lambda_rep"][j]),
              reads=[k.dres["diff_lambda_rep"]], writes=[lam.res])
        sm = k.sb(st, [128, 16], F32, "asm")
        lj = k.sb(st, [128, 2, 64], F32, "lamj")
        S.op("dve", lambda e: e.tensor_mul(out=lj[:, 0, :], in0=lam[:, 0, :], in1=lam[:, 1, :]), reads=[lam.res], writes=[lj.res])
        S.op("dve", lambda e: e.tensor_mul(out=lj[:, 1, :], in0=lam[:, 2, :], in1=lam[:, 3, :]), reads=[lam.res], writes=[lj.res])
        S.op("dve", lambda e: e.tensor_reduce(out=sm[:, 0:2], in_=lj[:, :, :], axis=AX.X, op=ALU.add), reads=[lj.res], writes=[sm.res])
        S.op("act", lambda e: e.activation(out=sm[:, 2:4], in_=sm[:, 0:2], func=AF.Exp), reads=[sm.res], writes=[sm.res])
        S.op("dve", lambda e: e.tensor_sub(out=sm[:, 4:5], in0=sm[:, 3:4], in1=sm[:, 2:3]), reads=[sm.res], writes=[sm.res])
        S.op("dve", lambda e: e.tensor_scalar_add(out=sm[:, 5:6], in0=sm[:, 4:5], scalar1=-lam_init), reads=[sm.res], writes=[sm.res])
        gsc = k.sb(st, [128, 128], F32, "gsc")
        S.dma("sp", lambda e: e.dma_start(out=gsc[:, :], in_=k.dr["diff_subln_g_rep"][j]),
              reads=[k.dres["diff_subln_g_rep"]], writes=[gsc.res])
        S.op("dve", lambda e: e.tensor_scalar_mul(out=gsc[:, :], in0=gsc[:, :], scalar1=1.0 - lam_init), reads=[gsc.res], writes=[gsc.res])
        for g in range(2):
            o, L = cfg.groups[g]
            QC = min(512, L)
            nsub = QC // 128
            nvar = nsub + 2
            nkt = L // 128
            with ExitStack() as st2:
                bk = k.sb(st2, [128, nvar, QC], F32, "bk")
                S.dma("sp", lambda e, bk=bk, g=g: e.dma_start(out=bk[:, :, :], in_=k.dr["bkt%d" % g][:, :, :]),
                      reads=[k.dres["bkt%d" % g]], writes=[bk.res])
                bacc = k.sb(st2, [128, 4, nvar, QC], F32, "bacc")
                bias = k.sb(st2, [128, 4, nvar, QC], BF16, "bias")
                msk = k.sbring(st2, 2, [128, nvar, QC], F32, "bmsk")
                S.op("dve", lambda e, bacc=bacc: e.memset(bacc[:, :, :, :], 0.0), writes=[bacc.res])
                for b in range(32):
                    mk = msk.next()
                    S.op("dve", lambda e, mk=mk, bk=bk, b=b: e.tensor_single_scalar(
                        out=mk[:, :, :], in_=bk[:, :, :], scalar=float(b), op=ALU.is_equal), reads=[bk.res], writes=[mk.res])
                    for h in range(4):
                        S.op("dve", lambda e, mk=mk, bacc=bacc, b=b, h=h: e.scalar_tensor_tensor(
                            out=bacc[:, h, :, :], in0=mk[:, :, :], scalar=tab[:, b * 4 + h:b * 4 + h + 1], in1=bacc[:, h, :, :],
                            op0=ALU.mult, op1=ALU.add), reads=[mk.res, tab.res, bacc.res], writes=[bacc.res])
                S.op("act", lambda e, bacc=bacc, bias=bias: e.activation(out=bias[:, :, :, :], in_=bacc[:, :, :, :], func=AF.Copy, scale=8.0),
                     reads=[bacc.res], writes=[bias.res])
                kt = k.sb(st2, [128, L], BF16, "akt")
                va = k.sb(st2, [128, nkt, 132], BF16, "ava")
                S.op("dve", lambda e, va=va: e.memset(va[:, :, 128:132], 1.0), writes=[va.res])
                qtr = k.sbring(st2, 2, [128, QC], BF16, "aqt")
                ptr_ = k.sbring(st2, 3, [128, QC], BF16, "apt")
                sps = Ring([k.ps(st2, [128, 512], F32, "asps") for _ in range(2)])
                ops_ = [k.ps(st2, [128, 512], F32, "aops") for _ in range(nsub)]
                oraw = k.sbring(st2, 2, [128, 2, nsub, 132], F32, "aoraw")
                wk_ = k.sbring(st2, 2, [128, 8], F32, "awk")
                t0r = k.sbring(st2, 2, [128, 128], F32, "at0")
                orr = k.sbring(st2, 2, [128, 128], F32, "aor")
                junk = k.sb(st2, [128, 128], F32, "ajunk")
                aout = k.sbring(st2, 2, [128, nsub, 128], BF16, "aout")
                for h in range(4):
                    S.dma("sp", lambda e, kt=kt, h=h, o=o, L=L: e.dma_start(out=kt[:, :], in_=k.dr["KT"][h * 128:(h + 1) * 128, o:o + L]),
                          reads=[k.dres["KT"]], writes=[kt.res])
                    S.dma("sp", lambda e, va=va, h=h, o=o, L=L: e.dma_start(
                        out=va[:, :, 0:128], in_=k.dr["V"][o:o + L, h * 128:(h + 1) * 128].rearrange("(t p) c -> p t c", p=128)),
                        reads=[k.dres["V"]], writes=[va.res])
                    for c in range(L // QC):
                        qt = qtr.next()
                        S.dma("sp", lambda e, qt=qt, h=h, q0=o + c * QC, QC=QC: e.dma_start(
                            out=qt[:, :], in_=k.dr["QT"][h * 128:(h + 1) * 128, q0:q0 + QC]),
                            reads=[k.dres["QT"]], writes=[qt.res])
                        orw = oraw.next()
                        for m in range(2):
                            for jk in range(nkt):
                                d = jk - c * nsub
                                near = -1 <= d <= nsub
                                sp_ = sps.next()
                                S.op("pe", lambda e, sp_=sp_, kt=kt, qt=qt, m=m, jk=jk, near=near, QC=QC: e.matmul(
                                    out=sp_[:, 0:QC], lhsT=kt[m * 64:(m + 1) * 64, jk * 128:(jk + 1) * 128],
                                    rhs=qt[m * 64:(m + 1) * 64, 0:QC], start=True, stop=not near),
                                    reads=[kt.res, qt.res], writes=[sp_.res])
                                if near:
                                    S.op("pe", lambda e, sp_=sp_, bias=bias, h=h, d=d, QC=QC: e.matmul(
                                        out=sp_[:, 0:QC], lhsT=C["ident_bf"][:, :], rhs=bias[:, h, d + 1, :],
                                        start=False, stop=True), reads=[bias.res, C["ident_bf"].res], writes=[sp_.res])
                                pt = ptr_.next()
                                if near:
                                    S.op("act", lambda e, pt=pt, sp_=sp_, QC=QC: e.activation(
                                        out=pt[:, 0:QC], in_=sp_[:, 0:QC], func=AF.Exp, scale=0.125),
                                        reads=[sp_.res], writes=[pt.res])
                                else:
                                    bcol = (15 if d < 0 else 31) * 4 + h
                                    S.op("act", lambda e, pt=pt, sp_=sp_, bcol=bcol, QC=QC: e.activation(
                                        out=pt[:, 0:QC], in_=sp_[:, 0:QC], func=AF.Exp, scale=0.125, bias=tab[:, bcol:bcol + 1]),
                                        reads=[sp_.res, tab.res], writes=[pt.res])
                                for sub in range(nsub):
                                    op_ = ops_[sub]
                                    S.op("pe", lambda e, op_=op_, pt=pt, va=va, sub=sub, jk=jk, nkt=nkt: e.matmul(
                                        out=op_[:, 0:129], lhsT=pt[:, sub * 128:(sub + 1) * 128], rhs=va[:, jk, 0:129],
                                        start=(jk == 0), stop=(jk == nkt - 1)), reads=[pt.res, va.res], writes=[op_.res])
                            for sub in range(nsub):
                                op_ = ops_[sub]
                                S.op("act" if sub % 2 else "dve", (lambda e, orw=orw, op_=op_, m=m, sub=sub: e.activation(
                                    out=orw[:, m, sub, 0:129], in_=op_[:, 0:129], func=AF.Copy)) if sub % 2 else
                                    (lambda e, orw=orw, op_=op_, m=m, sub=sub: e.tensor_copy(out=orw[:, m, sub, 0:129], in_=op_[:, 0:129])),
                                    reads=[op_.res], writes=[orw.res])
                        ao = aout.next()
                        for sub in range(nsub):
                            w = wk_.next()
                            S.op("dve", lambda e, w=w, orw=orw, sub=sub: e.reciprocal(out=w[:, 0:1], in_=orw[:, 0, sub, 128:129]), reads=[orw.res], writes=[w.res])
                            S.op("dve", lambda e, w=w, orw=orw, sub=sub: e.reciprocal(out=w[:, 1:2], in_=orw[:, 1, sub, 128:129]), reads=[orw.res], writes=[w.res])
                            S.op("dve", lambda e, w=w: e.tensor_mul(out=w[:, 2:3], in0=w[:, 1:2], in1=sm[:, 5:6]), reads=[w.res, sm.res], writes=[w.res])
                            t0 = t0r.next()
                            S.op("dve", lambda e, w=w, orw=orw, sub=sub, t0=t0: e.tensor_scalar_mul(out=t0[:, :], in0=orw[:, 0, sub, 0:128], scalar1=w[:, 0:1]),
                                 reads=[orw.res, w.res], writes=[t0.res])
                            oo = orr.next()
                            S.op("dve", lambda e, w=w, orw=orw, sub=sub, t0=t0, oo=oo: e.scalar_tensor_tensor(
                                out=oo[:, :], in0=orw[:, 1, sub, 0:128], scalar=w[:, 2:3], in1=t0[:, :], op0=ALU.mult, op1=ALU.add),
                                reads=[orw.res, w.res, t0.res], writes=[oo.res])
                            S.op("act", lambda e, w=w, oo=oo: e.activation(out=junk[:, :], in_=oo[:, :], func=AF.Square, accum_out=w[:, 3:4]),
                                 reads=[oo.res], writes=[junk.res, w.res])
                            S.op("dve", lambda e, w=w: e.tensor_scalar(out=w[:, 4:5], in0=w[:, 3:4], scalar1=1.0 / 128, scalar2=cfg.LN_EPS,
                                                                      op0=ALU.mult, op1=ALU.add), reads=[w.res], writes=[w.res])
                            S.op("act", lambda e, w=w: e.activation(out=w[:, 5:6], in_=w[:, 4:5], func=AF.Sqrt), reads=[w.res], writes=[w.res])
                            S.op("dve", lambda e, w=w: e.reciprocal(out=w[:, 6:7], in_=w[:, 5:6]), reads=[w.res], writes=[w.res])
                            S.op("dve", lambda e, w=w, oo=oo, ao=ao, sub=sub: e.scalar_tensor_tensor(
                                out=ao[:, sub, :], in0=oo[:, :], scalar=w[:, 6:7], in1=gsc[:, :], op0=ALU.mult, op1=ALU.mult),
                                reads=[oo.res, w.res, gsc.res], writes=[ao.res])
                        r0 = o + c * QC
                        S.dma("pool", lambda e, ao=ao, r0=r0, h=h, QC=QC: e.dma_start(
                            out=k.dr["MIX"][r0:r0 + QC, h * 128:(h + 1) * 128].rearrange("(s p) c -> p s c", p=128),
                            in_=ao[:, :, :]), reads=[ao.res], writes=[k.dres["MIX"]])
            S.barrier()
    S.barrier()


def fft_const_arrays(N1, nk_in, inverse, scale=1.0):
    c = fft_consts(N1, nk_in, inverse, scale)
    H = np.stack([c["Hre"], c["Him"], c["nHim"]], 2)
    return c["A"].astype(np.float32).astype(NPBF), c["B"].astype(np.float32).astype(NPBF), H.astype(np.float32).astype(NPBF)


def fft_s1(k, C, At, Bt, lhs_re, lhs_im, Wt, Kp, M, Ncol, CC, pring):
    S = k.S
    for c in range(CC):
        ps = pring.next()
        ap, rs = lhs_re(c)
        S.op("pe", lambda e, ps=ps, ap=ap: e.matmul(out=ps[0:M, 0:2 * Ncol], lhsT=ap, rhs=At[0:Kp, :],
                                                    start=True, stop=(lhs_im is None)),
             reads=[rs, At.res], writes=[ps.res])
        if lhs_im is not None:
            ap2, rs2 = lhs_im(c)
            S.op("pe", lambda e, ps=ps, ap2=ap2: e.matmul(out=ps[0:M, 0:2 * Ncol], lhsT=ap2, rhs=Bt[0:Kp, :],
                                                          start=False, stop=True),
                 reads=[rs2, Bt.res], writes=[ps.res])
        src = lambda ps=ps: ps[0:M, 0:2 * Ncol].rearrange("p (r n) -> p r n", r=2)
        if c % 2:
            S.op("act", lambda e, c=c, src=src: e.activation(out=Wt[0:M, :, :, c], in_=src(), func=AF.Copy),
                 reads=[ps.res], writes=[Wt.res])
        else:
            S.op("dve", lambda e, c=c, src=src: e.tensor_copy(out=Wt[0:M, :, :, c], in_=src()),
                 reads=[ps.res], writes=[Wt.res])


def fft_s2(k, C, hring, Hname, Wt, Kp2, Ncol, Mout, CC, real_only, emit, pring, SL=16):
    S = k.S
    for j0 in range(0, Ncol, SL):
        sl = min(SL, Ncol - j0)
        ht = hring.next()
        S.dma("sp", lambda e, ht=ht, j0=j0, sl=sl: e.dma_start(out=ht[0:Kp2, 0:sl, :, :], in_=k.dr[Hname][:, j0:j0 + sl, :, :]),
              reads=[k.dres[Hname]], writes=[ht.res])
        for j in range(j0, j0 + sl):
            jj = j - j0
            pre = pring.next()
            S.op("pe", lambda e, pre=pre, ht=ht, jj=jj, j=j: e.matmul(
                out=pre[0:Mout, 0:CC], lhsT=ht[0:Kp2, jj, 0, :], rhs=Wt[0:Kp2, 0, j, :], start=True, stop=False),
                reads=[ht.res, Wt.res], writes=[pre.res])
            S.op("pe", lambda e, pre=pre, ht=ht, jj=jj, j=j: e.matmul(
                out=pre[0:Mout, 0:CC], lhsT=ht[0:Kp2, jj, 2, :], rhs=Wt[0:Kp2, 1, j, :], start=False, stop=True),
                reads=[ht.res, Wt.res], writes=[pre.res])
            pim = None
            if not real_only:
                pim = pring.next()
                S.op("pe", lambda e, pim=pim, ht=ht, jj=jj, j=j: e.matmul(
                    out=pim[0:Mout, 0:CC], lhsT=ht[0:Kp2, jj, 1, :], rhs=Wt[0:Kp2, 0, j, :], start=True, stop=False),
                    reads=[ht.res, Wt.res], writes=[pim.res])
                S.op("pe", lambda e, pim=pim, ht=ht, jj=jj, j=j: e.matmul(
                    out=pim[0:Mout, 0:CC], lhsT=ht[0:Kp2, jj, 0, :], rhs=Wt[0:Kp2, 1, j, :], start=False, stop=True),
                    reads=[ht.res, Wt.res], writes=[pim.res])
            emit(j, pre, pim)


def phase_fnet(k, C, j):
    S, cfg = k.S, k.cfg
    CC = 64
    with ExitStack() as st:
        cs = k.sb(st, [128, 2, 512], BF16, "cdft")
        S.dma("sp", lambda e: e.dma_start(out=cs[:, :, :], in_=k.dr["cdft"][:, :].rearrange("(c p) n -> p c n", p=128)),
              reads=[k.dres["cdft"]], writes=[cs.res])
        xtr = k.sbring(st, 2, [128, 8, 128], BF16, "fxt")
        yr = k.sbring(st, 3, [128, 512], BF16, "fy")
        pp = Ring([k.ps(st, [128, 512], F32, "fpp") for _ in range(3)])
        for g in range(2):
            o, L = cfg.groups[g]
            for i in range(L // 128):
                xt = xtr.next()
                col = xt_col(cfg, g) + i * 128
                S.dma("sp", lambda e, xt=xt, col=col: e.dma_start(
                    out=xt[:, :, :], in_=k.dr["XT"][:, col:col + 128].rearrange("(c p) t -> p c t", p=128)),
                    reads=[k.dres["XT"]], writes=[xt.res])
                for cg in range(4):
                    ps = pp.next()
                    for h in range(2):
                        S.op("pe", lambda e, ps=ps, xt=xt, cg=cg, h=h: e.matmul(
                            out=ps[:, :], lhsT=xt[:, 2 * cg + h, :], rhs=cs[:, h, :], start=(h == 0), stop=(h == 1)),
                            reads=[xt.res, cs.res], writes=[ps.res])
                    y = yr.next()
                    S.op("act" if cg % 2 else "dve",
                         (lambda e, y=y, ps=ps: e.activation(out=y[:, :], in_=ps[:, :], func=AF.Copy)) if cg % 2 else
                         (lambda e, y=y, ps=ps: e.tensor_copy(out=y[:, :], in_=ps[:, :])), reads=[ps.res], writes=[y.res])
                    r0 = o + i * 128
                    S.dma("pool", lambda e, y=y, r0=r0, cg=cg: e.dma_start(
                        out=k.dr["YC"][:, r0:r0 + 128, cg * 256:(cg + 1) * 256].rearrange("r p c -> p r c"),
                        in_=y[:, :].rearrange("p (r c) -> p r c", r=2)), reads=[y.res], writes=[k.dres["YC"]])
    S.barrier()
    for g in range(2):
        o, L = cfg.groups[g]
        N1 = L // 128
        with ExitStack() as st:
            At = k.sb(st, [128, 2 * N1], BF16, "fA")
            Bt = k.sb(st, [128, 2 * N1], BF16, "fB")
            for t_, nm in ((At, "ffA%d" % g), (Bt, "ffB%d" % g)):
                S.dma("sp", lambda e, t_=t_, nm=nm: e.dma_start(out=t_[0:N1, :], in_=k.dr[nm][:, :]),
                      reads=[k.dres[nm]], writes=[t_.res])
            dre = k.sbring(st, 2, [128, 128, CC], BF16, "fdre")
            dim = k.sbring(st, 2, [128, 128, CC], BF16, "fdim")
            Wr = k.sbring(st, 2, [128, 2, N1, CC], BF16, "fW")
            fo = k.sbring(st, 2, [128, N1, CC], BF16, "ffo")
            hrF = k.sbring(st, 2, [128, 16, 3, 128], BF16, "fftHF")
            p1 = Ring([k.ps(st, [128, 512], F32, "fp1") for _ in range(3)])
            p2 = Ring([k.ps(st, [128, 512], F32, "fp2") for _ in range(3)])
            for ch in range(1024 // CC):
                c0 = ch * CC
                dr_, di_ = dre.next(), dim.next()
                for t_, ri in ((dr_, 0), (di_, 1)):
                    S.dma("sp", lambda e, t_=t_, ri=ri, c0=c0, o=o, L=L: e.dma_start(
                        out=t_[0:N1, :, :], in_=k.dr["YC"][ri, o:o + L, c0:c0 + CC].rearrange("(a b) c -> a b c", b=128)),
                        reads=[k.dres["YC"]], writes=[t_.res])
                Wt = Wr.next()
                fft_s1(k, C, At, Bt, lambda c, dr_=dr_: (dr_[0:N1, :, c], dr_.res), lambda c, di_=di_: (di_[0:N1, :, c], di_.res),
                       Wt, N1, 128, N1, CC, p1)
                f = fo.next()

                def emit(jx, pre, pim, f=f):
                    if jx % 2:
                        S.op("act", lambda e: e.activation(out=f[:, jx, :], in_=pre[:, 0:CC], func=AF.Copy),
                             reads=[pre.res], writes=[f.res])
                    else:
                        S.op("dve", lambda e: e.tensor_copy(out=f[:, jx, :], in_=pre[:, 0:CC]),
                             reads=[pre.res], writes=[f.res])
                fft_s2(k, C, hrF, "ffH%d" % g, Wt, 128, N1, 128, CC, True, emit, p2)
                S.dma("pool", lambda e, f=f, c0=c0, o=o, L=L, N1=N1: e.dma_start(
                    out=k.dr["MIX"][o:o + L, c0:c0 + CC].rearrange("(a b) c -> a b c", b=N1), in_=f[:, :, :]),
                    reads=[f.res], writes=[k.dres["MIX"]])
        S.barrier()


TWO_PI = 2.0 * math.pi
HCC = 32


def hyena_host_consts(cfg):
    c = {}
    for g in range(2):
        L = cfg.groups[g][1]
        n = np.arange(2 * L)
        pos = np.where(n < L, n, np.where(n == L, 0, 2 * L - n)).astype(np.float64)
        t = pos / (L - 1)
        wpos = 2 * np.pi * pos / L
        fr = np.linspace(1e-4, 15, 16)
        z = np.concatenate([t[:, None], np.cos(fr[None, :] * wpos[:, None]), -np.sin(fr[None, :] * wpos[:, None])], 1)
        c["hzT%d" % g] = np.ascontiguousarray(z.T).astype(np.float32)
        c["hntv%d" % g] = np.ascontiguousarray((-t).reshape(2 * L // 128, 128).T).astype(np.float32)
        m = np.ones(2 * L, np.float32)
        m[L] = 0.0
        c["hk2m%d" % g] = np.ascontiguousarray(m.reshape(2 * L // 128, 128).T)
        N1 = 2 * L // 128
        A, B, H = fft_const_arrays(N1, N1, False)
        c["hfA%d" % g], c["hfH%d" % g] = A, H
        A, B, H = fft_const_arrays(N1, N1 // 2, True, scale=1.0 / (2 * L))
        c["hiA%d" % g], c["hiB%d" % g], c["hiH%d" % g] = A, B, H
    return c


def declare_hyena(k):
    cfg = k.cfg
    NEV = (cfg.DEPTH + 1) // 2
    I = "ExternalInput"
    for g in range(2):
        L = cfg.groups[g][1]
        N1 = 2 * L // 128
        k.dram("hzT%d" % g, [33, 2 * L], F32, I)
        k.dram("hntv%d" % g, [128, 2 * L // 128], F32, I)
        k.dram("hk2m%d" % g, [128, 2 * L // 128], F32, I)
        k.dram("hfA%d" % g, [N1, 2 * N1], BF16, I)
        k.dram("hfH%d" % g, [128, N1, 3, 128], BF16, I)
        k.dram("hiA%d" % g, [128, 256], BF16, I)
        k.dram("hiB%d" % g, [128, 256], BF16, I)
        k.dram("hiH%d" % g, [N1, 128, 3, N1 // 2], BF16, I)
        k.dram("K2_%d" % g, [2 * L, 1024], BF16, "ExternalOutput" if getattr(cfg, "debug", False) else "Internal")
        k.dram("KF%d" % g, [1024 // HCC, 128, N1, 2, HCC], BF16)
    k.dram("hy_f_w1", [NEV, 33, 64], F32, I)
    k.dram("hy_f_w2", [NEV, 64, 64], F32, I)
    k.dram("hy_f_w3", [NEV, 64, 2048], F32, I)
    k.dram("hy_f_b1c", [NEV, 64, 1], F32, I)
    k.dram("hy_f_b2c", [NEV, 64, 1], F32, I)
    k.dram("hy_f_freqc", [NEV, 64, 2], F32, I)
    k.dram("hy_decay_rep", [NEV, 128, 2048], F32, I)
    k.dram("hy_skip_rep", [NEV, 2, 128, 512], F32, I)
    dbg = "ExternalOutput" if getattr(cfg, "debug", False) else "Internal"
    k.dram("YCV", [cfg.T, 512], BF16, dbg)
    k.dram("Z1", [cfg.T, 512], BF16, dbg)


def phase_hyena(k, C, j):
    S, cfg = k.S, k.cfg
    CC = HCC
    for g in range(2):
        o, L = cfg.groups[g]
        N1 = 2 * L // 128
        NT = 2 * L // 128
        with ExitStack() as stg:
            rs = k.sb(stg, [128, 1024], F32, "hrs")
            with ExitStack() as st:
                w1 = k.sb(st, [33, 64], F32, "hw1")
                w2 = k.sb(st, [64, 64], F32, "hw2")
                w3 = k.sb(st, [64, 2048], F32, "hw3")
                fq = k.sb(st, [64, 2], F32, "hfq")
                bc = k.sb(st, [64, 4], F32, "hbc")
                adec = k.sb(st, [128, 2048], F32, "hdec")
                ntv = k.sb(st, [128, NT], F32, "hntv")
                k2m = k.sb(st, [128, NT], F32, "hk2m")
                for t_, src in ((w1, k.dr["hy_f_w1"][j]), (w2, k.dr["hy_f_w2"][j]), (w3, k.dr["hy_f_w3"][j]),
                                (fq, k.dr["hy_f_freqc"][j]), (bc[:, 0:1], k.dr["hy_f_b1c"][j]), (bc[:, 1:2], k.dr["hy_f_b2c"][j]),
                                (adec, k.dr["hy_decay_rep"][j]), (ntv, k.dr["hntv%d" % g][:, :]), (k2m, k.dr["hk2m%d" % g][:, :])):
                    if isinstance(t_, Tile):
                        S.dma("sp", lambda e, t_=t_, src=src: e.dma_start(out=t_.t[tuple(slice(None) for _ in t_.t.shape)], in_=src), reads=[], writes=[t_.res])
                    else:
                        S.dma("sp", lambda e, t_=t_, src=src: e.dma_start(out=t_, in_=src), reads=[], writes=[bc.res])
                S.op("act", lambda e: e.activation(out=adec[:, :], in_=adec[:, :], func=AF.Abs),
                     reads=[adec.res], writes=[adec.res])
                S.op("dve", lambda e: e.tensor_mul(out=bc[:, 2:4], in0=bc[:, 0:2], in1=fq[:, 0:2]), reads=[bc.res, fq.res], writes=[bc.res])
                zr = k.sbring(st, 2, [33, 512], F32, "hz")
                ar = k.sbring(st, 2, [64, 512], F32, "ha")
                nr = k.sbring(st, 2, [64, 512], F32, "hn")
                ni = k.sbring(st, 2, [64, 512], I32, "hni")
                h1r = k.sbring(st, 2, [64, 512], F32, "hh1")
                h2r = k.sbring(st, 2, [64, 512], F32, "hh2")
                er = k.sbring(st, 2, [128, 1024], F32, "he")
                hr = k.sbring(st, 2, [128, 1024], F32, "hh")
                har = k.sbring(st, 2, [128, 1024], F32, "hha")
                hbr = k.sbring(st, 2, [128, 1024], BF16, "hhb")
                pA = Ring([k.ps(st, [128, 512], F32, "hpA") for _ in range(2)])
                pB = Ring([k.ps(st, [128, 512], F32, "hpB") for _ in range(2)])
                sacc = [k.ps(st, [128, 512], F32, "hsacc") for _ in range(2)]

                def sin_layer(ps, li, out):
                    a = ar.next()
                    S.op("dve", lambda e: e.tensor_scalar(out=a[:, :], in0=ps[0:64, :], scalar1=fq[:, li:li + 1],
                                                          scalar2=bc[:, 2 + li:3 + li], op0=ALU.mult, op1=ALU.add),
                         reads=[ps.res, fq.res, bc.res], writes=[a.res])
                    n_, nI = nr.next(), ni.next()
                    S.op("dve", lambda e: e.tensor_scalar_mul(out=n_[:, :], in0=a[:, :], scalar1=1.0 / TWO_PI), reads=[a.res], writes=[n_.res])
                    S.op("dve", lambda e: e.tensor_copy(out=nI[:, :], in_=n_[:, :]), reads=[n_.res], writes=[nI.res])
                    S.op("dve", lambda e: e.tensor_copy(out=n_[:, :], in_=nI[:, :]), reads=[nI.res], writes=[n_.res])
                    S.op("dve", lambda e: e.scalar_tensor_tensor(out=a[:, :], in0=n_[:, :], scalar=-TWO_PI, in1=a[:, :],
                                                                 op0=ALU.mult, op1=ALU.add), reads=[n_.res, a.res], writes=[a.res])
                    S.op("dve", lambda e: e.tensor_single_scalar(out=n_[:, :], in_=a[:, :], scalar=math.pi, op=ALU.is_gt), reads=[a.res], writes=[n_.res])
                    S.op("dve", lambda e: e.scalar_tensor_tensor(out=a[:, :], in0=n_[:, :], scalar=-TWO_PI, in1=a[:, :],
                                                                 op0=ALU.mult, op1=ALU.add), reads=[n_.res, a.res], writes=[a.res])
                    S.op("dve", lambda e: e.tensor_single_scalar(out=n_[:, :], in_=a[:, :], scalar=-math.pi, op=ALU.is_lt), reads=[a.res], writes=[n_.res])
                    S.op("dve", lambda e: e.scalar_tensor_tensor(out=a[:, :], in0=n_[:, :], scalar=TWO_PI, in1=a[:, :],
                                                                 op0=ALU.mult, op1=ALU.add), reads=[n_.res, a.res], writes=[a.res])
                    S.op("act", lambda e: e.activation(out=out[:, :], in_=a[:, :], func=AF.Sin), reads=[a.res], writes=[out.res])

                BLK = min(512, 2 * L)
                tile_i = 0
                for b0 in range(0, 2 * L, BLK):
                    zt = zr.next()
                    S.dma("sp", lambda e, zt=zt, b0=b0: e.dma_start(out=zt[:, 0:BLK], in_=k.dr["hzT%d" % g][:, b0:b0 + BLK]),
                          reads=[], writes=[zt.res])
                    ps = pA.next()
                    S.op("pe", lambda e, ps=ps, zt=zt: e.matmul(out=ps[0:64, 0:BLK], lhsT=w1[:, :], rhs=zt[:, 0:BLK], start=True, stop=True),
                         reads=[w1.res, zt.res], writes=[ps.res])
                    h1 = h1r.next()
                    sin_layer(ps, 0, h1)
                    ps = pA.next()
                    S.op("pe", lambda e, ps=ps, h1=h1: e.matmul(out=ps[0:64, 0:BLK], lhsT=w2[:, :], rhs=h1[:, 0:BLK], start=True, stop=True),
                         reads=[w2.res, h1.res], writes=[ps.res])
                    h2 = h2r.next()
                    sin_layer(ps, 1, h2)
                    for tj in range(BLK // 128):
                        tg = b0 // 128 + tj
                        d = 0 if tg * 128 < L else 1
                        ee = er.next()
                        S.op("act", lambda e, ee=ee, d=d, tg=tg: e.activation(out=ee[:, :], in_=adec[:, d * 1024:(d + 1) * 1024],
                                                                               func=AF.Exp, scale=ntv[:, tg:tg + 1]),
                             reads=[adec.res, ntv.res], writes=[ee.res])
                        hh = hr.next()
                        for hf in range(2):
                            p3 = pB.next()
                            S.op("pe", lambda e, p3=p3, h2=h2, tj=tj, d=d, hf=hf: e.matmul(
                                out=p3[:, :], lhsT=h2[:, tj * 128:(tj + 1) * 128],
                                rhs=w3[:, d * 1024 + hf * 512:d * 1024 + (hf + 1) * 512], start=True, stop=True),
                                reads=[h2.res, w3.res], writes=[p3.res])
                            S.op("dve", lambda e, p3=p3, hh=hh, ee=ee, hf=hf: e.tensor_mul(
                                out=hh[:, hf * 512:(hf + 1) * 512], in0=p3[:, :], in1=ee[:, hf * 512:(hf + 1) * 512]),
                                reads=[p3.res, ee.res], writes=[hh.res])
                        ha = har.next()
                        S.op("act", lambda e, ha=ha, hh=hh: e.activation(out=ha[:, :], in_=hh[:, :], func=AF.Abs),
                             reads=[hh.res], writes=[ha.res])
                        for hf in range(2):
                            S.op("pe", lambda e, ha=ha, hf=hf, tile_i=tile_i: e.matmul(
                                out=sacc[hf][:, :], lhsT=C["ones_f"][:, :], rhs=ha[:, hf * 512:(hf + 1) * 512],
                                start=(tile_i == 0), stop=(tile_i == NT - 1)), reads=[ha.res, C["ones_f"].res], writes=[sacc[hf].res])
                        hb = hbr.next()
                        S.op("dve", lambda e, hb=hb, hh=hh, tg=tg: e.tensor_scalar_mul(out=hb[:, :], in0=hh[:, :], scalar1=k2m[:, tg:tg + 1]),
                             reads=[hh.res, k2m.res], writes=[hb.res])
                        S.dma("pool", lambda e, hb=hb, tg=tg: e.dma_start(out=k.dr["K2_%d" % g][tg * 128:(tg + 1) * 128, :], in_=hb[:, :]),
                              reads=[hb.res], writes=[k.dres["K2_%d" % g]])
                        tile_i += 1
                for hf in range(2):
                    S.op("dve", lambda e, hf=hf: e.reciprocal(out=rs[:, hf * 512:(hf + 1) * 512], in_=sacc[hf][:, :]),
                         reads=[sacc[hf].res], writes=[rs.res])
            S.barrier()
            with ExitStack() as st:
                At = k.sb(st, [128, 2 * N1], BF16, "hA")
                S.dma("sp", lambda e: e.dma_start(out=At[0:N1, :], in_=k.dr["hfA%d" % g][:, :]), reads=[], writes=[At.res])
                dat = k.sbring(st, 2, [128, 128, CC], BF16, "hkd")
                Wr = k.sbring(st, 2, [128, 2, N1, CC], BF16, "hkW")
                kfo = k.sbring(st, 2, [128, N1, 2, CC], BF16, "hkfo")
                hrK = k.sbring(st, 2, [128, 16, 3, 128], BF16, "fftHK")
                p1 = Ring([k.ps(st, [128, 512], F32, "hkp1") for _ in range(3)])
                p2 = Ring([k.ps(st, [128, 512], F32, "hkp2") for _ in range(4)])
                for ch in range(1024 // CC):
                    c0 = ch * CC
                    d_ = dat.next()
                    S.dma("sp", lambda e, d_=d_, c0=c0: e.dma_start(
                        out=d_[0:N1, :, :], in_=k.dr["K2_%d" % g][:, c0:c0 + CC].rearrange("(a b) c -> a b c", b=128)),
                        reads=[k.dres["K2_%d" % g]], writes=[d_.res])
                    Wt = Wr.next()
                    fft_s1(k, C, At, None, lambda c, d_=d_: (d_[0:N1, :, c], d_.res), None, Wt, N1, 128, N1, CC, p1)
                    kf = kfo.next()

                    def emit(jx, pre, pim, kf=kf, c0=c0):
                        S.op("dve", lambda e: e.tensor_mul(out=kf[:, jx, 0, :], in0=pre[:, 0:CC], in1=rs[:, c0:c0 + CC]),
                             reads=[pre.res, rs.res], writes=[kf.res])
                        S.op("dve", lambda e: e.tensor_mul(out=kf[:, jx, 1, :], in0=pim[:, 0:CC], in1=rs[:, c0:c0 + CC]),
                             reads=[pim.res, rs.res], writes=[kf.res])
                    fft_s2(k, C, hrK, "hfH%d" % g, Wt, 128, N1, 128, CC, False, emit, p2)
                    S.dma("pool", lambda e, kf=kf, ch=ch: e.dma_start(out=k.dr["KF%d" % g][ch], in_=kf[:, :, :, :]),
                          reads=[kf.res], writes=[k.dres["KF%d" % g]])
            S.barrier()
            for order in range(2):
                with ExitStack() as st:
                    At = k.sb(st, [128, 2 * N1], BF16, "hcA")
                    iA = k.sb(st, [128, 256], BF16, "hciA")
                    iB = k.sb(st, [128, 256], BF16, "hciB")
                    S.dma("sp", lambda e: e.dma_start(out=At[0:N1, :], in_=k.dr["hfA%d" % g][:, :]), reads=[], writes=[At.res])
                    S.dma("sp", lambda e: e.dma_start(out=iA[:, :], in_=k.dr["hiA%d" % g][:, :]), reads=[], writes=[iA.res])
                    S.dma("sp", lambda e: e.dma_start(out=iB[:, :], in_=k.dr["hiB%d" % g][:, :]), reads=[], writes=[iB.res])
                    dat = k.sbring(st, 2, [128, 128, CC], BF16, "hcd")
                    Wr = k.sbring(st, 1, [128, 2, N1, CC], BF16, "hcW")
                    Yr = k.sbring(st, 1, [128, 2, N1, CC], BF16, "hcY")
                    Br = k.sbring(st, 1, [128, 2, 128, CC], BF16, "hcB")
                    kfr = k.sbring(st, 2, [128, N1, 2, CC], BF16, "hckf")
                    yor = k.sbring(st, 2, [128, 128, CC], BF16, "hcyo")
                    tm = k.sbring(st, 4, [128, CC], F32, "hctm")
                    hrA = k.sbring(st, 2, [128, 16, 3, 128], BF16, "fftHA")
                    hrB = k.sbring(st, 2, [128, 16, 3, max(1, N1 // 2)], BF16, "fftHB")
                    p1 = Ring([k.ps(st, [128, 512], F32, "hcp1") for _ in range(3)])
                    p2 = Ring([k.ps(st, [128, 512], F32, "hcp2") for _ in range(4)])
                    for ch in range(512 // CC):
                        c0 = ch * CC
                        d_ = dat.next()
                        if order == 0:
                            src = k.dr["U"][0, o:o + L, c0:c0 + CC]
                            sres = k.dres["U"]
                        else:
                            src = k.dr["Z1"][o:o + L, c0:c0 + CC]
                            sres = k.dres["Z1"]
                        S.dma("sp", lambda e, d_=d_, src=src: e.dma_start(
                            out=d_[0:N1 // 2, :, :], in_=src.rearrange("(a b) c -> a b c", b=128)), reads=[sres], writes=[d_.res])
                        kf = kfr.next()
                        S.dma("sp", lambda e, kf=kf, ch=ch: e.dma_start(out=kf[:, :, :, :], in_=k.dr["KF%d" % g][order * (512 // CC) + ch]),
                              reads=[k.dres["KF%d" % g]], writes=[kf.res])
                        Wt, Yt, Bt_ = Wr.next(), Yr.next(), Br.next()
                        fft_s1(k, C, At, None, lambda c, d_=d_: (d_[0:N1 // 2, :, c], d_.res), None, Wt, N1 // 2, 128, N1, CC, p1)

                        def emit(jx, pre, pim, kf=kf, Yt=Yt):
                            t1, t2 = tm.next(), tm.next()
                            S.op("dve", lambda e: e.tensor_mul(out=t1[:, :], in0=pre[:, 0:CC], in1=kf[:, jx, 0, :]), reads=[pre.res, kf.res], writes=[t1.res])
                            S.op("dve", lambda e: e.tensor_mul(out=t2[:, :], in0=pim[:, 0:CC], in1=kf[:, jx, 1, :]), reads=[pim.res, kf.res], writes=[t2.res])
                            S.op("dve", lambda e: e.tensor_sub(out=Yt[:, 0, jx, :], in0=t1[:, :], in1=t2[:, :]), reads=[t1.res, t2.res], writes=[Yt.res])
                            t3, t4 = tm.next(), tm.next()
                            S.op("dve", lambda e: e.tensor_mul(out=t3[:, :], in0=pre[:, 0:CC], in1=kf[:, jx, 1, :]), reads=[pre.res, kf.res], writes=[t3.res])
                            S.op("dve", lambda e: e.tensor_mul(out=t4[:, :], in0=pim[:, 0:CC], in1=kf[:, jx, 0, :]), reads=[pim.res, kf.res], writes=[t4.res])
                            S.op("dve", lambda e: e.tensor_add(out=Yt[:, 1, jx, :], in0=t3[:, :], in1=t4[:, :]), reads=[t3.res, t4.res], writes=[Yt.res])
                        fft_s2(k, C, hrA, "hfH%d" % g, Wt, 128, N1, 128, CC, False, emit, p2)
                        fft_s1(k, C, iA, iB, lambda c, Yt=Yt: (Yt[:, 0, :, c], Yt.res), lambda c, Yt=Yt: (Yt[:, 1, :, c], Yt.res),
                               Bt_, 128, N1, 128, CC, p1)
                        yo = yor.next()

                        def emit2(jx, pre, pim, yo=yo):
                            if jx % 2:
                                S.op("act", lambda e: e.activation(out=yo[0:N1 // 2, jx, :], in_=pre[0:N1 // 2, 0:CC], func=AF.Copy),
                                     reads=[pre.res], writes=[yo.res])
                            else:
                                S.op("dve", lambda e: e.tensor_copy(out=yo[0:N1 // 2, jx, :], in_=pre[0:N1 // 2, 0:CC]),
                                     reads=[pre.res], writes=[yo.res])
                        fft_s2(k, C, hrB, "hiH%d" % g, Bt_, N1, 128, N1 // 2, CC, True, emit2, p2)
                        S.dma("pool", lambda e, yo=yo, c0=c0: e.dma_start(
                            out=k.dr["YCV"][o:o + L, c0:c0 + CC].rearrange("(a b) c -> a b c", b=128), in_=yo[0:N1 // 2, :, :]),
                            reads=[yo.res], writes=[k.dres["YCV"]])
                S.barrier()
                with ExitStack() as st:
                    sk = k.sb(st, [128, 512], F32, "hsk")
                    S.dma("sp", lambda e: e.dma_start(out=sk[:, :], in_=k.dr["hy_skip_rep"][j, order]), reads=[], writes=[sk.res])
                    yr_ = k.sbring(st, 2, [128, 512], BF16, "hgy")
                    zr_ = k.sbring(st, 2, [128, 512], BF16, "hgz")
                    gr_ = k.sbring(st, 2, [128, 512], BF16, "hgg")
                    tr_ = k.sbring(st, 2, [128, 512], F32, "hgt")
                    or_ = k.sbring(st, 2, [128, 512], BF16, "hgo")
                    for i in range(L // 128):
                        r0 = o + i * 128
                        y, z, gt_ = yr_.next(), zr_.next(), gr_.next()
                        S.dma("sp", lambda e, y=y, r0=r0: e.dma_start(out=y[:, :], in_=k.dr["YCV"][r0:r0 + 128, :]), reads=[k.dres["YCV"]], writes=[y.res])
                        zsrc = k.dr["U"][0, r0:r0 + 128, :] if order == 0 else k.dr["Z1"][r0:r0 + 128, :]
                        S.dma("sp", lambda e, z=z, zsrc=zsrc: e.dma_start(out=z[:, :], in_=zsrc),
                              reads=[k.dres["U"] if order == 0 else k.dres["Z1"]], writes=[z.res])
                        S.dma("sp", lambda e, gt_=gt_, r0=r0: e.dma_start(out=gt_[:, :], in_=k.dr["U"][1 + order, r0:r0 + 128, :]),
                              reads=[k.dres["U"]], writes=[gt_.res])
                        t = tr_.next()
                        S.op("dve", lambda e, t=t, z=z: e.tensor_mul(out=t[:, :], in0=z[:, :], in1=sk[:, :]), reads=[z.res, sk.res], writes=[t.res])
                        S.op("dve", lambda e, t=t, y=y: e.tensor_add(out=t[:, :], in0=t[:, :], in1=y[:, :]), reads=[t.res, y.res], writes=[t.res])
                        oo = or_.next()
                        S.op("dve", lambda e, t=t, gt_=gt_, oo=oo: e.tensor_mul(out=oo[:, :], in0=t[:, :], in1=gt_[:, :]), reads=[t.res, gt_.res], writes=[oo.res])
                        if order == 0:
                            S.dma("pool", lambda e, oo=oo, r0=r0: e.dma_start(out=k.dr["Z1"][r0:r0 + 128, :], in_=oo[:, :]),
                                  reads=[oo.res], writes=[k.dres["Z1"]])
                        else:
                            S.dma("pool", lambda e, oo=oo, r0=r0: e.dma_start(out=k.dr["MIX"][r0:r0 + 128, 512:1024], in_=oo[:, :]),
                                  reads=[oo.res], writes=[k.dres["MIX"]])
                S.barrier()
```
